# Optimizing a Trainium2 kernel written in Bass

```python
import math
import jax, jax.numpy as jnp
from jax import lax
import numpy as np

D_MODEL = 1024
BATCH = 4
SEQ = 8192
DEPTH = 4

N_META = 16
BLOCK = 128
WINDOW = 128
ATT_HEADS = 8
ATT_KV_HEADS = 2
HEAD_DIM = 64
SSM_HEADS = 16
SSM_HEAD_DIM = 64
SSM_INNER = SSM_HEADS * SSM_HEAD_DIM
SSM_GROUPS = 2
SSM_STATE = 64
SSM_CONV = 4
CONF_DIM = D_MODEL
CONF_KERNEL = 31
D_FF = 4 * D_MODEL
EPS = 1e-6
LN_EPS = 1e-5

Q_W = ATT_HEADS * HEAD_DIM
KV_W = ATT_KV_HEADS * HEAD_DIM
BC_W = SSM_GROUPS * SSM_STATE
XBC_W = SSM_INNER + 2 * BC_W
S_Q = Q_W
S_K = S_Q + KV_W
S_V = S_K + KV_W
S_Z = S_V + SSM_INNER
S_XBC = S_Z + XBC_W
IN_W = S_XBC + SSM_HEADS
MIX_W = Q_W + SSM_INNER
N_EVEN = (DEPTH + 1) // 2
N_ODD = DEPTH // 2
FRONT_PAD = BLOCK - N_META

kernel_name = "hybrid_swa_ssd_conformer_trunk"


def rms_norm(x, g):
    xf = x.astype(jnp.float32)
    y = xf * lax.rsqrt(jnp.mean(xf * xf, -1, keepdims=True) + EPS)
    return (y * g.astype(jnp.float32)).astype(x.dtype)


def layer_norm(x, g, b):
    xf = x.astype(jnp.float32)
    mu = jnp.mean(xf, -1, keepdims=True)
    var = jnp.mean(jnp.square(xf - mu), -1, keepdims=True)
    y = (xf - mu) * lax.rsqrt(var + LN_EPS)
    return (y * g.astype(jnp.float32) + b.astype(jnp.float32)).astype(x.dtype)


def causal_depthwise_conv(x, w, b):
    k = w.shape[0]
    y = lax.conv_general_dilated(x, w[:, None, :].astype(x.dtype), window_strides=(1,),
                                 padding=((k - 1, 0),), dimension_numbers=('NWC', 'WIO', 'NWC'),
                                 feature_group_count=x.shape[-1])
    return y + b.astype(x.dtype)


def front_pad(a, pad):
    return jnp.pad(a, [(0, 0), (pad, 0)] + [(0, 0)] * (a.ndim - 2))


def alibi_slopes(n):
    return jnp.exp2(-8.0 * jnp.arange(1, n + 1, dtype=jnp.float32) / n)


def sliding_window_attention(q, k, v, sinks):
    bsz = q.shape[0]
    grp = ATT_HEADS // ATT_KV_HEADS
    qb = front_pad(q, FRONT_PAD).reshape(bsz, -1, BLOCK, ATT_KV_HEADS, grp, HEAD_DIM)
    kb = front_pad(k, FRONT_PAD).reshape(bsz, -1, BLOCK, ATT_KV_HEADS, HEAD_DIM)
    vb = front_pad(v, FRONT_PAD).reshape(bsz, -1, BLOCK, ATT_KV_HEADS, HEAD_DIM)
    nb = kb.shape[1]
    shift = ((0, 0), (1, 0), (0, 0), (0, 0), (0, 0))
    kprev = jnp.pad(kb, shift)[:, :-1]
    vprev = jnp.pad(vb, shift)[:, :-1]
    kmeta = jnp.broadcast_to(k[:, None, :N_META], (bsz, nb, N_META, ATT_KV_HEADS, HEAD_DIM))
    vmeta = jnp.broadcast_to(v[:, None, :N_META], (bsz, nb, N_META, ATT_KV_HEADS, HEAD_DIM))
    keys = jnp.concatenate([kmeta, kprev, kb], axis=2)
    vals = jnp.concatenate([vmeta, vprev, vb], axis=2)
    nk = N_META + 2 * BLOCK
    q_pos = jnp.arange(nb)[:, None] * BLOCK + jnp.arange(BLOCK)[None, :] - FRONT_PAD
    meta_pos = jnp.broadcast_to(jnp.arange(N_META)[None, :], (nb, N_META))
    k_pos = jnp.concatenate([meta_pos, q_pos - BLOCK, q_pos], axis=1)
    is_meta = (jnp.arange(nk) < N_META)[None, None, :]
    dist = q_pos[:, :, None] - k_pos[:, None, :]
    valid = jnp.where(is_meta, dist >= 0,
                      (k_pos[:, None, :] >= N_META) & (dist >= 0) & (dist < WINDOW))
    pen_dist = jnp.where(is_meta, jnp.minimum(jnp.abs(dist), WINDOW), jnp.abs(dist)).astype(jnp.float32)
    slopes = alibi_slopes(ATT_HEADS).reshape(ATT_KV_HEADS, grp)
    bias = -slopes[None, :, :, None, None] * pen_dist[:, None, None]
    s = jnp.einsum('bnqkgd,bnskd->bnkgqs', qb, keys).astype(jnp.float32) * (HEAD_DIM ** -0.5)
    s = jnp.where(valid[:, None, None], s + bias, -1e30)
    sink = jnp.broadcast_to(sinks.astype(jnp.float32).reshape(ATT_KV_HEADS, grp)[None, None, :, :, None, None],
                            s.shape[:-1] + (1,))
    p = jax.nn.softmax(jnp.concatenate([s, sink], axis=-1), axis=-1)[..., :-1]
    o = jnp.einsum('bnkgqs,bnskd->bnqkgd', p.astype(vals.dtype), vals)
    return o.reshape(bsz, nb * BLOCK, Q_W)[:, FRONT_PAD:]


def segsum(a):
    cs = jnp.cumsum(a, axis=-1)
    d = cs[..., :, None] - cs[..., None, :]
    t = a.shape[-1]
    return jnp.where(jnp.tril(jnp.ones((t, t), bool)), d, -jnp.inf)


def ssd_scan(x, dt, a, b_mat, c_mat):
    out_dtype = x.dtype
    f32 = jnp.float32
    bsz, lp = x.shape[:2]
    nc = lp // BLOCK
    hpg = SSM_HEADS // SSM_GROUPS
    xc = x.astype(f32).reshape(bsz, nc, BLOCK, SSM_GROUPS, hpg, SSM_HEAD_DIM)
    dtc = dt.astype(f32).reshape(bsz, nc, BLOCK, SSM_GROUPS, hpg)
    bc = b_mat.astype(f32).reshape(bsz, nc, BLOCK, SSM_GROUPS, SSM_STATE)
    cc = c_mat.astype(f32).reshape(bsz, nc, BLOCK, SSM_GROUPS, SSM_STATE)
    dt_t = jnp.moveaxis(dtc, 2, -1)
    adt = dt_t * a.astype(f32).reshape(SSM_GROUPS, hpg)[None, None, :, :, None]
    a_cs = jnp.cumsum(adt, axis=-1)
    cb = jnp.einsum('bclgn,bcsgn->bcgls', cc, bc)
    w = cb[:, :, :, None] * jnp.exp(segsum(adt)) * dt_t[:, :, :, :, None, :]
    y_diag = jnp.einsum('bcghls,bcsghp->bclghp', w, xc)
    wx = jnp.exp(a_cs[..., -1:] - a_cs) * dt_t
    states = jnp.einsum('bclgn,bcghl,bclghp->bcghpn', bc, wx, xc)
    chunk_decay = jnp.exp(a_cs[..., -1])

    def step(h, inp):
        st, dec = inp
        return h * dec[..., None, None] + st, h

    h0 = jnp.zeros((bsz, SSM_GROUPS, hpg, SSM_HEAD_DIM, SSM_STATE), f32)
    _, prev = lax.scan(step, h0, (jnp.moveaxis(states, 1, 0), jnp.moveaxis(chunk_decay, 1, 0)))
    prev = jnp.moveaxis(prev, 0, 1)
    y_off = jnp.einsum('bclgn,bcghpn,bcghl->bclghp', cc, prev, jnp.exp(a_cs))
    y = (y_diag + y_off).reshape(bsz, lp, SSM_HEADS, SSM_HEAD_DIM)
    return y.astype(out_dtype)


def mamba2_branch(z, xbc, dt_raw, conv_w, conv_b, dt_bias, a_log, d_skip, norm_w):
    bsz, seqlen = z.shape[:2]
    xbc = jax.nn.silu(causal_depthwise_conv(xbc, conv_w, conv_b))
    xs, bm, cm = jnp.split(xbc, [SSM_INNER, SSM_INNER + BC_W], axis=-1)
    xs = xs.reshape(bsz, seqlen, SSM_HEADS, SSM_HEAD_DIM)
    bm = bm.reshape(bsz, seqlen, SSM_GROUPS, SSM_STATE)
    cm = cm.reshape(bsz, seqlen, SSM_GROUPS, SSM_STATE)
    dt = jax.nn.softplus(dt_raw + dt_bias.astype(dt_raw.dtype))
    a = -jnp.exp(a_log.astype(jnp.float32))
    y = ssd_scan(front_pad(xs, FRONT_PAD), front_pad(dt, FRONT_PAD), a,
                 front_pad(bm, FRONT_PAD), front_pad(cm, FRONT_PAD))[:, FRONT_PAD:]
    y = y + xs * d_skip.astype(xs.dtype)[:, None]
    y = y.reshape(bsz, seqlen, SSM_INNER) * jax.nn.silu(z)
    yg = y.astype(jnp.float32).reshape(bsz, seqlen, SSM_GROUPS, SSM_INNER // SSM_GROUPS)
    yg = yg * lax.rsqrt(jnp.mean(yg * yg, -1, keepdims=True) + EPS)
    return (yg.reshape(bsz, seqlen, SSM_INNER) * norm_w.astype(jnp.float32)).astype(z.dtype)


def even_mixer(h, norm_g, w_in, conv_w, conv_b, dt_bias, a_log, d_skip, ssm_norm_w, q_norm, k_norm, sinks, w_out):
    bsz, seqlen = h.shape[:2]
    u = rms_norm(h, norm_g)
    proj = u @ w_in
    q, k, v, z, xbc, dt_raw = jnp.split(proj, [S_Q, S_K, S_V, S_Z, S_XBC], axis=-1)
    q = rms_norm(q.reshape(bsz, seqlen, ATT_HEADS, HEAD_DIM), q_norm)
    k = rms_norm(k.reshape(bsz, seqlen, ATT_KV_HEADS, HEAD_DIM), k_norm)
    v = v.reshape(bsz, seqlen, ATT_KV_HEADS, HEAD_DIM)
    att = sliding_window_attention(q, k, v, sinks)
    ssm = mamba2_branch(z, xbc, dt_raw, conv_w, conv_b, dt_bias, a_log, d_skip, ssm_norm_w)
    return jnp.concatenate([att, ssm], axis=-1) @ w_out


def conformer_conv_module(h, norm_g, pw1_w, pw1_b, dw_w, dw_b, ln_g, ln_b, pw2_w, pw2_b):
    u = rms_norm(h, norm_g) @ pw1_w + pw1_b
    u = u[..., :CONF_DIM] * jax.nn.sigmoid(u[..., CONF_DIM:])
    u = causal_depthwise_conv(u, dw_w, dw_b)
    u = jax.nn.silu(layer_norm(u, ln_g, ln_b))
    return u @ pw2_w + pw2_b


def sq_relu_mlp(h, norm_g, w_up, w_down):
    return jnp.square(jax.nn.relu(rms_norm(h, norm_g) @ w_up)) @ w_down


def setup_inputs(seed: int = 0) -> dict:
    key = jax.random.key(seed)
    ks = iter(jax.random.split(key, 40))

    def nrm(shape, scale):
        return scale * jax.random.normal(next(ks), shape, jnp.float32)

    def gain(shape):
        return 1.0 + nrm(shape, 0.02)

    x = nrm((BATCH, SEQ, D_MODEL), 1.0)
    meta_tokens = nrm((N_META, D_MODEL), 1.0)
    mix_norm_even = gain((N_EVEN, D_MODEL))
    w_in = nrm((N_EVEN, D_MODEL, IN_W), D_MODEL ** -0.5)
    ssm_conv_w = nrm((N_EVEN, SSM_CONV, XBC_W), SSM_CONV ** -0.5)
    ssm_conv_b = nrm((N_EVEN, XBC_W), 0.02)
    dt0 = jnp.exp(jax.random.uniform(next(ks), (N_EVEN, SSM_HEADS), jnp.float32,
                                     minval=math.log(1e-3), maxval=math.log(1e-1)))
    dt_bias = dt0 + jnp.log(-jnp.expm1(-dt0))
    a_log = jnp.log(jax.random.uniform(next(ks), (N_EVEN, SSM_HEADS), jnp.float32, minval=1.0, maxval=16.0))
    d_skip = gain((N_EVEN, SSM_HEADS))
    ssm_norm_w = gain((N_EVEN, SSM_INNER))
    q_norm = gain((N_EVEN, HEAD_DIM))
    k_norm = gain((N_EVEN, HEAD_DIM))
    sinks = nrm((N_EVEN, ATT_HEADS), 0.5)
    w_out = nrm((N_EVEN, MIX_W, D_MODEL), MIX_W ** -0.5)
    mix_norm_odd = gain((N_ODD, D_MODEL))
    pw1_w = nrm((N_ODD, D_MODEL, 2 * CONF_DIM), D_MODEL ** -0.5)
    pw1_b = nrm((N_ODD, 2 * CONF_DIM), 0.02)
    dw_w = nrm((N_ODD, CONF_KERNEL, CONF_DIM), CONF_KERNEL ** -0.5)
    dw_b = nrm((N_ODD, CONF_DIM), 0.02)
    ln_g = gain((N_ODD, CONF_DIM))
    ln_b = nrm((N_ODD, CONF_DIM), 0.02)
    pw2_w = nrm((N_ODD, CONF_DIM, D_MODEL), CONF_DIM ** -0.5)
    pw2_b = nrm((N_ODD, D_MODEL), 0.02)
    mlp_norm = gain((DEPTH, D_MODEL))
    w_up = nrm((DEPTH, D_MODEL, D_FF), D_MODEL ** -0.5)
    w_down = nrm((DEPTH, D_FF, D_MODEL), D_FF ** -0.5)
    return {"x": x, "meta_tokens": meta_tokens, "mix_norm_even": mix_norm_even, "w_in": w_in,
            "ssm_conv_w": ssm_conv_w, "ssm_conv_b": ssm_conv_b, "dt_bias": dt_bias, "a_log": a_log,
            "d_skip": d_skip, "ssm_norm_w": ssm_norm_w, "q_norm": q_norm, "k_norm": k_norm,
            "sinks": sinks, "w_out": w_out, "mix_norm_odd": mix_norm_odd, "pw1_w": pw1_w,
            "pw1_b": pw1_b, "dw_w": dw_w, "dw_b": dw_b, "ln_g": ln_g, "ln_b": ln_b,
            "pw2_w": pw2_w, "pw2_b": pw2_b, "mlp_norm": mlp_norm, "w_up": w_up, "w_down": w_down}


def reference(x, meta_tokens, mix_norm_even, w_in, ssm_conv_w, ssm_conv_b, dt_bias, a_log, d_skip,
              ssm_norm_w, q_norm, k_norm, sinks, w_out, mix_norm_odd, pw1_w, pw1_b, dw_w, dw_b,
              ln_g, ln_b, pw2_w, pw2_b, mlp_norm, w_up, w_down):
    bsz = x.shape[0]
    meta = jnp.broadcast_to(meta_tokens[None].astype(x.dtype), (bsz, N_META, D_MODEL))
    h = jnp.concatenate([meta, x], axis=1)
    for layer in range(DEPTH):
        i = layer // 2
        if layer % 2 == 0:
            h = h + even_mixer(h, mix_norm_even[i], w_in[i], ssm_conv_w[i], ssm_conv_b[i], dt_bias[i],
                               a_log[i], d_skip[i], ssm_norm_w[i], q_norm[i], k_norm[i], sinks[i], w_out[i])
        else:
            h = h + conformer_conv_module(h, mix_norm_odd[i], pw1_w[i], pw1_b[i], dw_w[i], dw_b[i],
                                          ln_g[i], ln_b[i], pw2_w[i], pw2_b[i])
        h = h + sq_relu_mlp(h, mlp_norm[layer], w_up[layer], w_down[layer])
    return h[:, N_META:]
```

```python
import contextlib
import os
import numpy as np
import concourse.bass as bass
import concourse.mybir as mybir
from concourse.bass_utils import run_bass_kernel_spmd

F32 = mybir.dt.float32
BF16 = mybir.dt.bfloat16
AF = mybir.ActivationFunctionType
ALU = mybir.AluOpType

D = 1024
RGROUPS = [[0, 1], [2, 3], [4, 5], [6, 7]]
KDBG = int(os.environ.get('KDBG', '99'))
NSLOT = 4
EPS = 1e-6
LN_EPS = 1e-5


class Buf:
    __slots__ = ("name", "w", "r")

    def __init__(self, name=""):
        self.name = name
        self.w = None
        self.r = {}


class Prog:
    ENGS = ("pe", "act", "dve", "pool", "sp")

    def __init__(self, nc):
        self.nc = nc
        self.q = {e: [] for e in self.ENGS}
        self.cnt = {e: 0 for e in self.ENGS}
        self.waited = {e: {} for e in self.ENGS}
        self.dma_cnt = {}
        self.sem_keys = set(self.ENGS)

    def _deps(self, eng, reads, writes):
        evs = []
        for b in reads:
            if b.w is not None:
                evs.append(("raw", b.w))
        for b in writes:
            if b.w is not None:
                evs.append(("waw", b.w))
            for ev in b.r.values():
                evs.append(("war", ev))
        waits = {}
        for kind, (weng, key, val) in evs:
            if weng == eng:
                if val > self.cnt[eng]:
                    continue
            if self.waited[eng].get(key, 0) >= val:
                continue
            if waits.get(key, 0) < val:
                waits[key] = val
        for key, val in waits.items():
            self.waited[eng][key] = val
        return list(waits.items())

    def _commit(self, ev, ekey, reads, writes):
        for b in writes:
            b.w = ev
            b.r = {}
        for b in reads:
            b.r[ekey] = ev

    def op(self, eng, fn, reads=(), writes=(), inc=True):
        waits = self._deps(eng, reads, writes)
        if inc:
            self.cnt[eng] += 1
            val = self.cnt[eng]
        else:
            val = self.cnt[eng] + 1
        ev = (eng, eng, val)
        self.q[eng].append((waits, fn, (eng, 1) if inc else None))
        self._commit(ev, eng, reads, writes)
        return ev

    def dma(self, eng, sem_key, fn, reads=(), writes=()):
        self.sem_keys.add(sem_key)
        waits = self._deps(eng, reads, writes)
        n = self.dma_cnt.get(sem_key, 0) + 1
        self.dma_cnt[sem_key] = n
        ev = (None, sem_key, 16 * n)
        self.q[eng].append((waits, fn, (sem_key, 16)))
        self._commit(ev, ("dma", sem_key), reads, writes)
        return ev

    def coll(self, eng, sem_key, fn, reads=(), writes=()):
        self.sem_keys.add(sem_key)
        waits = self._deps(eng, reads, writes)
        n = self.dma_cnt.get(sem_key, 0) + 1
        self.dma_cnt[sem_key] = n
        ev = (None, sem_key, n)
        self.q[eng].append((waits, fn, (sem_key, 1)))
        self._commit(ev, ("dma", sem_key), reads, writes)
        return ev

    def wait_all(self, eng, bufs):
        waits = self._deps(eng, bufs, ())
        self.q[eng].append((waits, None, None))

    def emit(self):
        nc = self.nc
        keys = sorted(self.sem_keys, key=str)
        with contextlib.ExitStack() as st:
            sems = {}
            for i, k in enumerate(keys):
                sems[k] = st.enter_context(nc.semaphore("sm%d" % i))
            block = st.enter_context(nc.Block())
            handles = {"pe": block.tensor, "act": block.scalar, "dve": block.vector,
                       "pool": block.gpsimd, "sp": block.sync}

            def make(e):
                def body(engobj):
                    for waits, fn, inc in self.q[e]:
                        for key, val in waits:
                            engobj.wait_ge(sems[key], val)
                        if fn is None:
                            continue
                        ins = fn(engobj)
                        if inc is not None:
                            ins.then_inc(sems[inc[0]], inc[1])
                return body

            for e in self.ENGS:
                if self.q[e]:
                    handles[e](make(e))


def _vec_layout():
    vc = {}
    n = 0

    def add(name, k):
        nonlocal n
        vc[name] = n
        n += k

    for i in range(2):
        add("e_g%d" % i, 8)
        add("e_cw%d" % i, 40)
        add("e_cb%d" % i, 10)
        add("e_ds%d" % i, 8)
        add("e_nw%d" % i, 8)
        add("e_qn%d" % i, 4)
        add("e_kn%d" % i, 1)
    for i in range(2):
        add("o_g%d" % i, 8)
        add("o_b1%d" % i, 16)
        add("o_dw%d" % i, 31 * 8)
        add("o_db%d" % i, 8)
        add("o_lg%d" % i, 8)
        add("o_lb%d" % i, 8)
        add("o_b2%d" % i, 8)
    for l in range(4):
        add("m_g%d" % l, 8)
    return vc, n


VC, NV = _vec_layout()
NR = 2 * 40
NCST = 5 * 128
NTAB = 5 * 8 * 128
SD = 4 + 20 + 2 + 512
PIECES_PER_TILE = 27 + 22


def build(nsteps, nlayers=4):
    nc = bass.Bass("TRN2", target_bir_lowering=False)
    tiles = [(-3 + 4 * s_, 4) for s_ in range(nsteps)]
    nblk_seq = 4 * nsteps

    xin = nc.dram_tensor("xin", [nblk_seq * 128, D], F32, kind="ExternalInput").ap()
    sdat_d = nc.dram_tensor("sdat", [nsteps, 128, SD], F32, kind="ExternalInput").ap()
    wall = nc.dram_tensor("wall", [PIECES_PER_TILE, 128, 4096], F32, kind="ExternalInput").ap()
    vecs_d = nc.dram_tensor("vecs", [128, NV], F32, kind="ExternalInput").ap()
    rows_d = nc.dram_tensor("rows", [128, NR], F32, kind="ExternalInput").ap()
    cst_d = nc.dram_tensor("cst", [128, NCST], F32, kind="ExternalInput").ap()
    tab_d = nc.dram_tensor("tab", [128, NTAB], F32, kind="ExternalInput").ap()
    yout = nc.dram_tensor("y", [nblk_seq * 128, D], F32, kind="ExternalOutput").ap()

    xsrc = nc.dram_tensor("xsrc", [512, D], F32).ap()
    xdst = nc.dram_tensor("xdst", [1024, D], F32).ap()

    P = Prog(nc)
    st = contextlib.ExitStack()
    with st:
        def sb(name, shape, dt):
            return st.enter_context(nc.sbuf_tensor(name, shape, dt))

        H = sb("H", [128, 8, 512], F32)
        BIGF = sb("BIGF", [128, 8192], F32)
        XPG = sb("XPG", [128, 4096], F32)
        QK32 = sb("QK32", [128, 6, 512], F32)
        U = sb("U", [128, 8, 512], BF16)
        ring = sb("ring", [128, NSLOT, 4096], BF16)
        Q16 = sb("Q16", [128, 4, 512], BF16)
        K16 = sb("K16", [128, 2, 512], BF16)
        VT = sb("VT", [128, 4, 128], BF16)
        BC16 = sb("BC16", [128, 2, 512], BF16)
        XTM = sb("XTM", [128, 1024], BF16)
        BTM = sb("BTM", [128, 128], BF16)
        XDT = sb("XDT", [128, 1024], BF16)
        XW = sb("XW", [128, 1024], BF16)
        E16 = sb("E16", [128, 2, 1024], BF16)
        PT = sb("PT", [128, 3, 1024], BF16)
        WT = sb("WT", [128, 2, 1024], BF16)
        C2 = sb("C2", [128, 1024], BF16)
        RD = sb("RD", [128, 512], F32)
        SM = sb("SM", [128, 8, 16], F32)
        DEC2 = sb("DEC2", [128, 8], F32)
        SQ = sb("SQ", [128, 2, 512], F32)
        RS = sb("RS", [128, 2, 512], F32)
        TMPF = sb("TMPF", [128, 2, 512], F32)
        KPREV = sb("KPREV", [128, 2, 2, 128], BF16)
        KMETA = sb("KMETA", [128, 2, 2, 128], BF16)
        VPREV = sb("VPREV", [128, 2, 128], BF16)
        VMETA = sb("VMETA", [128, 2, 128], BF16)
        XPC = sb("XPC", [128, 2, 10, 3], BF16)
        XP16 = sb("XP16", [128, 10, 515], BF16)
        ST32 = sb("ST32", [128, 2, 512], F32)
        ST16 = sb("ST16", [128, 2, 512], BF16)
        AHC = sb("AHC", [128, 2, 8, 30], BF16)
        DGR = sb("DGR", [128, 2, 8, 128], BF16)
        CST = sb("CST", [128, NCST], F32)
        TAB = sb("TAB", [128, NTAB], BF16)
        C16 = sb("C16", [128, 3, 128], BF16)
        VEC = sb("VEC", [128, NV], F32)
        ROWS = sb("ROWS", [128, NR], F32)
        AROW = sb("AROW", [128, 2, 16], F32)
        ESK = sb("ESK", [128, 2, 8], F32)
        SDT = sb("SDT", [128, SD], F32)
        TME = sb("TME", [128, 1024], BF16)

        PSD = [st.enter_context(nc.psum_tensor("psd%d" % i, [128, 1024], F32)) for i in range(4)]
        b_ps = [Buf("ps%d" % i) for i in range(8)]

        def bank(i):
            return PSD[i // 2][:, (i % 2) * 512:(i % 2) * 512 + 512]

        HID = BIGF[:, 0:8192].bitcast(BF16).rearrange("p (c t) -> p c t", c=32)
        SZ = BIGF[:, 0:4096].rearrange("p (c t) -> p c t", c=8)
        XS = BIGF[:, 4096:8192].rearrange("p (c t) -> p c t", c=8)
        AH16 = BIGF[:, 0:2168].bitcast(BF16).rearrange("p (c t) -> p c t", c=8)
        CO = BIGF[:, 4096:8192].rearrange("p (c t) -> p c t", c=8)
        G32 = XPG[:, 0:4096].rearrange("p (c t) -> p c t", c=8)
        XT = XPG[:, 0:4096].rearrange("p (b f) -> p b f", b=4)
        MIX = QK32[:, :, :].rearrange("p c t -> p (c t)").bitcast(BF16).rearrange("p (c t) -> p c t", c=12)
        OUTST = U[:, :, :].rearrange("p c t -> p (c t)").bitcast(F32).rearrange("p (s f) -> p s f", s=2)
        ident = CST[:, 0:128]
        tri = CST[:, 128:256]
        ones32 = CST[:, 256:384]
        onesblk = CST[:, 384:512]
        ones16 = C16[:, 0, :]
        neg16 = C16[:, 1, :]
        ident16 = C16[:, 2, :]
        TABv = TAB[:, :].rearrange("p (a h q) -> p a h q", a=5, h=8)

        b_H = [Buf("H%d" % c) for c in range(8)]
        b_U = [Buf("U%d" % c) for c in range(8)]
        b_BIG = [Buf("BIG%d" % c) for c in range(32)]
        b_XP = [Buf("XP%d" % c) for c in range(10)]
        b_G = [Buf("G%d" % c) for c in range(8)]
        b_XT = Buf("XT")
        ALLX = b_G + [b_XT]
        b_QK = [Buf("QK%d" % c) for c in range(12)]
        b_Q16 = [Buf() for _ in range(4)]
        b_K16 = [Buf() for _ in range(2)]
        b_VT = [Buf() for _ in range(4)]
        b_BC = [Buf() for _ in range(2)]
        b_XTM, b_BTM, b_XDT, b_XW, b_C2, b_RD, b_DEC2 = (Buf() for _ in range(7))
        b_E16 = [Buf(), Buf()]
        b_PT = [Buf(), Buf(), Buf()]
        b_WT = [Buf(), Buf()]
        b_SM = [Buf() for _ in range(8)]
        b_SQ = [Buf(), Buf()]
        b_RS = [Buf(), Buf()]
        b_TMPF = [Buf(), Buf()]
        b_KPREV, b_KMETA, b_VPREV, b_VMETA = ([Buf(), Buf()] for _ in range(4))
        b_XPC = [Buf(), Buf()]
        b_ST32 = [Buf(), Buf()]
        b_ST16 = [Buf(), Buf()]
        b_AHC = [Buf(), Buf()]
        b_c0, b_c1, b_c2 = Buf("c0"), Buf("c1"), Buf("c2")
        CSA = [b_c0, b_c1, b_c2]
        b_yout = Buf("yout")
        b_SDT = Buf("sdt")
        b_SRC = Buf("xsrc")
        b_DST = Buf("xdst")
        b_TME = Buf("tme")

        def bigs(lo, hi):
            return b_BIG[lo // 256:(hi + 255) // 256]

        def ACT(out, in_, func, reads, writes, **kw):
            return P.op("act", lambda e: e.activation(out=out, in_=in_, func=func, **kw), reads, writes)

        def TT(eng, out, in0, in1, op, reads, writes):
            return P.op(eng, lambda e: e.tensor_tensor(out=out, in0=in0, in1=in1, op=op), reads, writes)

        def TS(eng, out, in0, s1, s2, op0, op1, reads, writes):
            if s2 is None:
                return P.op(eng, lambda e: e.tensor_scalar(out=out, in0=in0, scalar1=s1, scalar2=None, op0=op0), reads, writes)
            return P.op(eng, lambda e: e.tensor_scalar(out=out, in0=in0, scalar1=s1, scalar2=s2, op0=op0, op1=op1), reads, writes)

        def STT(out, in0, scalar, in1, op0, op1, reads, writes):
            return P.op("dve", lambda e: e.scalar_tensor_tensor(out=out, in0=in0, scalar=scalar, in1=in1, op0=op0, op1=op1), reads, writes)

        def COPY(eng, out, in_, reads, writes):
            if eng == "act":
                return P.op("act", lambda e: e.activation(out=out, in_=in_, func=AF.Copy), reads, writes)
            return P.op(eng, lambda e: e.tensor_copy(out=out, in_=in_), reads, writes)

        def MM(out, lhsT, rhs, start, stop, reads, writes, inc, tp=None):
            if tp is None:
                return P.op("pe", lambda e: e.matmul(out, lhsT=lhsT, rhs=rhs, start=start, stop=stop), reads, writes, inc=inc)
            return P.op("pe", lambda e: e.matmul(out, lhsT=lhsT, rhs=rhs, start=start, stop=stop, tile_position=tp), reads, writes, inc=inc)

        def TR(out, in_, idm, reads, writes, inc):
            return P.op("pe", lambda e: e.transpose(out, in_, idm), reads, writes, inc=inc)

        def MEMSET(eng, ap, val, writes):
            return P.op(eng, lambda e: e.memset(ap, val), (), writes)

        b_DG = [Buf("dg0"), Buf("dg1")]
        dg_rr = [0]

        def dg_batch(wtaps, n):
            k = dg_rr[0] % 2
            dg_rr[0] += 1
            TT("dve", DGR[:, k, 0:n, :], C16[:, 2:3, :].to_broadcast([128, n, 128]), wtaps.unsqueeze(2).to_broadcast([128, n, 128]), ALU.mult,
               CSA, [b_DG[k]])
            return DGR[:, k, :, :], b_DG[k]

        acc_rr = [0]

        def acc():
            i = acc_rr[0] % 3
            acc_rr[0] += 1
            return bank(i), b_ps[i]

        class WStream:
            def __init__(self, total):
                self.total = total
                self.issued = 0
                self.next = 0
                self.bufs = [Buf("ring%d" % i) for i in range(NSLOT)]

            def _issue(self):
                j = self.issued
                s = j % NSLOT
                src = wall[j % PIECES_PER_TILE]
                P.dma("pool", ("w", s), lambda e: e.dma_start(out=ring[:, s, :], in_=src), (), [self.bufs[s]])
                self.issued += 1

            def get(self):
                j = self.next
                self.next += 1
                while self.issued < min(self.total, j + NSLOT):
                    self._issue()
                return ring[:, j % NSLOT, :], self.bufs[j % NSLOT]

            def skip(self, n):
                for _ in range(n):
                    self.get()

        W = WStream(PIECES_PER_TILE * len(tiles))

        P.dma("sp", "c0", lambda e: e.dma_start(out=CST[:, :], in_=cst_d[:, :]), (), [b_c0])
        P.dma("sp", "c0", lambda e: e.dma_start(out=VEC[:, :], in_=vecs_d[:, :]), (), [b_c0])
        P.dma("sp", "c0", lambda e: e.dma_start(out=ROWS[:, :], in_=rows_d[:, :]), (), [b_c0])
        P.dma("pool", "c1", lambda e: e.dma_start(out=TAB[:, :], in_=tab_d[:, :]), (), [b_c1])
        P.dma("pool", "c1", lambda e: e.dma_start(out=C16[:, 0, :], in_=cst_d[:, 256:384]), (), [b_c1])
        P.dma("pool", "c1", lambda e: e.dma_start(out=C16[:, 1, :], in_=cst_d[:, 512:640]), (), [b_c1])
        P.dma("pool", "c1", lambda e: e.dma_start(out=C16[:, 2, :], in_=cst_d[:, 0:128]), (), [b_c1])
        for i in range(2):
            ACT(AROW[:, i, :], ROWS[:, i * 40 + 16:i * 40 + 32], AF.Exp, [b_c0], [b_c2])
            ACT(ESK[:, i, :], ROWS[:, i * 40 + 32:i * 40 + 40], AF.Exp, [b_c0], [b_c2])
        TS("dve", AROW[:, :, :], AROW[:, :, :], -1.0, None, ALU.mult, None, [b_c2], [b_c2])
        MEMSET("pool", ST32[:, :, :], 0.0, b_ST32)
        MEMSET("pool", ST16[:, :, :], 0.0, b_ST16)
        MEMSET("pool", XPC[:, :, :, :], 0.0, b_XPC)
        MEMSET("pool", AHC[:, :, :, :], 0.0, b_AHC)
        MEMSET("pool", KMETA[:, :, :, :], 0.0, b_KMETA)
        MEMSET("pool", VMETA[:, :, :], 0.0, b_VMETA)
        MEMSET("pool", KPREV[:, :, :, :], 0.0, b_KPREV)
        MEMSET("pool", VPREV[:, :, :], 0.0, b_VPREV)

        def vcol(name, c):
            k = VC[name] + c
            return VEC[:, k:k + 1]

        def load_x(ti):
            b0, nb = tiles[ti]
            src = xin[ti * 512:(ti + 1) * 512, :].rearrange("(b p) f -> p b f", p=128)
            P.dma("sp", "xin", lambda e: e.dma_start(out=XT[:, 0:nb, :], in_=src), (), ALLX)

        XT2 = BIGF[:, 0:4096].rearrange("p (b f) -> p b f", b=4)

        def load_exchange(ti):
            src = xdst[0:512, :].rearrange("(b p) f -> p b f", p=128)
            P.dma("sp", "xdl", lambda e: e.dma_start(out=XT2[:, :, :], in_=src), [b_DST], bigs(0, 4096))
            for blk in range(4):
                STT(XT[:, blk, :], XT2[:, blk, :], SDT[:, 24:25], XT[:, blk, :], ALU.mult, ALU.add, bigs(0, 4096) + [b_SDT, b_XT], [b_XT])

        def load_sdat(ti):
            P.dma("sp", "sdat", lambda e: e.dma_start(out=SDT[:, :], in_=sdat_d[ti]), (), [b_SDT])

        def transpose_in(ti):
            b0, nb = tiles[ti]
            k = 0
            for blk in range(nb):
                for half in range(2):
                    ps, pb = acc()
                    for j in range(4):
                        c = half * 4 + j
                        TR(ps[:, j * 128:(j + 1) * 128], XT[:, blk, c * 128:(c + 1) * 128], ident, CSA + [b_XT], [pb], inc=(j == 3))
                    COPY("act" if k % 2 == 0 else "dve", H[:, half * 4:half * 4 + 4, blk * 128:(blk + 1) * 128],
                         ps[:, 0:512].rearrange("p (c t) -> p c t", c=4), [pb], b_H[half * 4:half * 4 + 4])
                    k += 1

        def store_out(ti):
            b0, nb = tiles[ti]
            k = 0
            for blk in range(nb):
                gb = b0 + 3 + blk
                slot = k % 2
                for half in range(2):
                    ps, pb = acc()
                    for j in range(4):
                        c = half * 4 + j
                        TR(ps[:, j * 128:(j + 1) * 128], H[:, c, blk * 128:(blk + 1) * 128], ident, [*CSA, b_H[c]], [pb], inc=(j == 3))
                    COPY("act" if half == 0 else "dve", OUTST[:, slot, half * 512:(half + 1) * 512], ps[:, 0:512], [pb], b_U[slot * 4:slot * 4 + 4])
                dst = yout[gb * 128:(gb + 1) * 128, :]
                P.dma("sp", ("yo", slot), (lambda e, dst=dst, slot=slot: e.dma_start(out=dst, in_=OUTST[:, slot, :])), b_U[slot * 4:slot * 4 + 4], [b_yout])
                dst2 = xsrc[blk * 128:(blk + 1) * 128, :]
                P.dma("sp", ("ys", slot), (lambda e, dst2=dst2, slot=slot: e.dma_start(out=dst2, in_=OUTST[:, slot, :])), b_U[slot * 4:slot * 4 + 4], [b_SRC])
                k += 1
            P.wait_all("pool", [b_SRC])
            P.coll("pool", "cc", lambda e: e.collective_compute("AllGather", ALU.bypass, replica_groups=RGROUPS, ins=[xsrc.opt()], outs=[xdst.opt()]),
                   [b_SRC], [b_DST])

        def rstd_from(ps, pb, T, inv_n, eps, slot):
            ACT(RS[:, slot, 0:T], ps[:, 0:T], AF.Ln, [pb], [b_RS[slot]], bias=eps, scale=inv_n)
            ACT(RS[:, slot, 0:T], RS[:, slot, 0:T], AF.Exp, [b_RS[slot]], [b_RS[slot]], scale=-0.5)

        def rmsnorm(T, gname):
            ps, pb = bank(3), b_ps[3]
            for c in range(8):
                s = c % 2
                ACT(SQ[:, s, 0:T], H[:, c, 0:T], AF.Square, [b_H[c]], [b_SQ[s]])
                MM(ps[:, 0:T], ones32, SQ[:, s, 0:T], c == 0, c == 7, [*CSA, b_SQ[s]], [pb], inc=True)
            rstd_from(ps, pb, T, 1.0 / D, EPS, 0)
            for c in range(8):
                STT(U[:, c, 0:T], H[:, c, 0:T], vcol(gname, c), RS[:, 0, 0:T], ALU.mult, ALU.mult, [b_H[c], *CSA, b_RS[0]], [b_U[c]])

        def mlp(T, l):
            rmsnorm(T, "m_g%d" % l)
            for j in range(8):
                slot, wb = W.get()
                for fc in range(4):
                    cf = j * 4 + fc
                    ps, pb = acc()
                    for kc in range(8):
                        MM(ps[:, 0:T], slot[:, kc * 512 + fc * 128:kc * 512 + fc * 128 + 128], U[:, kc, 0:T], kc == 0, kc == 7,
                           [wb, b_U[kc]], [pb], inc=(kc == 7))
                    s = cf % 2
                    ACT(TMPF[:, s, 0:T], ps[:, 0:T], AF.Relu, [pb], [b_TMPF[s]])
                    TT("dve", HID[:, cf, 0:T], TMPF[:, s, 0:T], TMPF[:, s, 0:T], ALU.mult, [b_TMPF[s]], [b_BIG[cf]])
            for oc in range(8):
                slot, wb = W.get()
                ps, pb = acc()
                for kc in range(32):
                    MM(ps[:, 0:T], slot[:, kc * 128:(kc + 1) * 128], HID[:, kc, 0:T], kc == 0, kc == 31, [wb, b_BIG[kc]], [pb], inc=(kc == 31))
                TT("dve", H[:, oc, 0:T], H[:, oc, 0:T], ps[:, 0:T], ALU.add, [pb, b_H[oc]], [b_H[oc]])

        def conformer(T, i, ti):
            gb0 = tiles[ti][0]
            rmsnorm(T, "o_g%d" % i)
            ah_b = bigs(0, 2176)
            co_b = bigs(4096, 8192)
            COPY("pool", AH16[:, :, 0:30], AHC[:, i, :, :], [b_AHC[i]], ah_b)
            for j in range(4):
                slot, wb = W.get()
                for fc in range(4):
                    c = (j % 2) * 4 + fc
                    ps, pb = acc()
                    for kc in range(8):
                        MM(ps[:, 0:T], slot[:, kc * 512 + fc * 128:kc * 512 + fc * 128 + 128], U[:, kc, 0:T], kc == 0, kc == 7,
                           [wb, b_U[kc]], [pb], inc=(kc == 7))
                    if j < 2:
                        STT(CO[:, c, 0:T], ps[:, 0:T], vcol("o_b1%d" % i, c), SDT[:, 26:26 + T], ALU.add, ALU.mult, [pb, *CSA, b_SDT], co_b)
                    else:
                        s = c % 2
                        ACT(TMPF[:, s, 0:T], ps[:, 0:T], AF.Sigmoid, [pb, *CSA], [b_TMPF[s]], bias=vcol("o_b1%d" % i, 8 + c), scale=1.0)
                        TT("dve", AH16[:, c, 30:30 + T], CO[:, c, 0:T], TMPF[:, s, 0:T], ALU.mult, [b_TMPF[s]] + co_b, ah_b)
            dwb = VC["o_dw%d" % i]
            for c in range(8):
                ps, pb = acc()
                for j0 in range(0, 31, 8):
                    n = min(8, 31 - j0)
                    dg, dgb = dg_batch(VEC[:, dwb + c + 8 * j0:dwb + c + 8 * (j0 + n):8], n)
                    for jj in range(n):
                        j = j0 + jj
                        MM(ps[:, 0:T], dg[:, jj, :], AH16[:, c, j:j + T], j == 0, j == 30, [dgb] + ah_b, [pb], inc=(jj == n - 1))
                ACT(CO[:, c, 0:T], ps[:, 0:T], AF.Identity, [pb, *CSA], co_b, bias=vcol("o_db%d" % i, c), scale=1.0)
            COPY("pool", AHC[:, i, :, :], AH16[:, :, T:T + 30], ah_b, [b_AHC[i]])
            ps1, pb1 = bank(3), b_ps[3]
            ps2, pb2 = acc()
            for c in range(8):
                s = c % 2
                MM(ps1[:, 0:T], ones32, CO[:, c, 0:T], c == 0, c == 7, CSA + co_b, [pb1], inc=True)
                ACT(SQ[:, s, 0:T], CO[:, c, 0:T], AF.Square, co_b, [b_SQ[s]])
                MM(ps2[:, 0:T], ones32, SQ[:, s, 0:T], c == 0, c == 7, [*CSA, b_SQ[s]], [pb2], inc=True)
            TS("dve", RS[:, 1, 0:T], ps1[:, 0:T], 1.0 / D, None, ALU.mult, None, [pb1], [b_RS[1]])
            TT("dve", SQ[:, 0, 0:T], RS[:, 1, 0:T], RS[:, 1, 0:T], ALU.mult, [b_RS[1]], [b_SQ[0]])
            STT(SQ[:, 1, 0:T], ps2[:, 0:T], 1.0 / D, SQ[:, 0, 0:T], ALU.mult, ALU.subtract, [pb2, b_SQ[0]], [b_SQ[1]])
            ACT(RS[:, 0, 0:T], SQ[:, 1, 0:T], AF.Ln, [b_SQ[1]], [b_RS[0]], bias=LN_EPS, scale=1.0)
            ACT(RS[:, 0, 0:T], RS[:, 0, 0:T], AF.Exp, [b_RS[0]], [b_RS[0]], scale=-0.5)
            for c in range(8):
                s = c % 2
                TT("dve", TMPF[:, s, 0:T], CO[:, c, 0:T], RS[:, 1, 0:T], ALU.subtract, co_b + [b_RS[1]], [b_TMPF[s]])
                TT("pool", TMPF[:, s, 0:T], TMPF[:, s, 0:T], RS[:, 0, 0:T], ALU.mult, [b_TMPF[s], b_RS[0]], [b_TMPF[s]])
                ACT(U[:, c, 0:T], TMPF[:, s, 0:T], AF.Silu, [b_TMPF[s], *CSA], [b_U[c]], bias=vcol("o_lb%d" % i, c), scale=vcol("o_lg%d" % i, c))
            for j in range(2):
                slot, wb = W.get()
                for fc in range(4):
                    oc = j * 4 + fc
                    ps, pb = acc()
                    for kc in range(8):
                        MM(ps[:, 0:T], slot[:, kc * 512 + fc * 128:kc * 512 + fc * 128 + 128], U[:, kc, 0:T], kc == 0, kc == 7,
                           [wb, b_U[kc]], [pb], inc=(kc == 7))
                    STT(H[:, oc, 0:T], ps[:, 0:T], vcol("o_b2%d" % i, oc), H[:, oc, 0:T], ALU.add, ALU.add, [pb, *CSA, b_H[oc]], [b_H[oc]])

        def even_mixer(T, i, ti):
            gb0, nb = tiles[ti]
            rmsnorm(T, "e_g%d" % i)
            sz_b = bigs(0, 4096)
            xs_b = bigs(4096, 8192)
            slot, wb = W.get()
            for fc in range(4):
                ps, pb = acc()
                for kc in range(8):
                    MM(ps[:, 0:T], slot[:, kc * 512 + fc * 128:kc * 512 + fc * 128 + 128], U[:, kc, 0:T], kc == 0, kc == 7, [wb, b_U[kc]], [pb], inc=(kc == 7))
                COPY("act", QK32[:, fc, 0:T], ps[:, 0:T], [pb], b_QK[2 * fc:2 * fc + 2])
            slot, wb = W.get()
            for kv in range(2):
                ps, pb = acc()
                for kc in range(8):
                    MM(ps[:, 0:T], slot[:, kc * 512 + kv * 128:kc * 512 + kv * 128 + 128], U[:, kc, 0:T], kc == 0, kc == 7, [wb, b_U[kc]], [pb], inc=(kc == 7))
                COPY("dve", QK32[:, 4 + kv, 0:T], ps[:, 0:T], [pb], b_QK[8 + 2 * kv:10 + 2 * kv])
            vd_ps, vd_pb = bank(3), b_ps[3]
            for b in range(nb):
                for kc in range(8):
                    MM(vd_ps[:, b * 128:b * 128 + 128 + 0], U[:, kc, b * 128:(b + 1) * 128], slot[:, kc * 512 + 256:kc * 512 + 384], kc == 0, kc == 7,
                       [wb, b_U[kc]], [vd_pb], inc=(kc == 7))
                COPY("act", VT[:, b, :], vd_ps[:, b * 128:b * 128 + 128], [vd_pb], [b_VT[b]])
            dt_ps, dt_pb = acc()
            for b in range(nb):
                for kc in range(8):
                    MM(dt_ps[:, b * 16:b * 16 + 16], U[:, kc, b * 128:(b + 1) * 128], slot[:, kc * 512 + 384:kc * 512 + 400], kc == 0, kc == 7,
                       [wb, b_U[kc]], [dt_pb], inc=(kc == 7))
            DTALL = RS[:, 1, 0:64].rearrange("p (b h) -> p b h", b=4)
            TT("dve", DTALL[:, 0:nb, :], dt_ps[:, 0:nb * 16].rearrange("p (b h) -> p b h", b=nb),
               ROWS[:, i * 40:i * 40 + 16].unsqueeze(1).to_broadcast([128, nb, 16]), ALU.add, [dt_pb, *CSA], [b_RS[1]])
            ACT(DTALL[:, 0:nb, :], DTALL[:, 0:nb, :], AF.Exp, [b_RS[1]], [b_RS[1]])
            ACT(DTALL[:, 0:nb, :], DTALL[:, 0:nb, :], AF.Ln, [b_RS[1]], [b_RS[1]], bias=1.0, scale=1.0)
            TT("dve", DTALL[:, 0:nb, :], DTALL[:, 0:nb, :], SDT[:, 0:nb].unsqueeze(2).to_broadcast([128, nb, 16]), ALU.mult, [b_RS[1], b_SDT], [b_RS[1]])
            for j in range(2):
                slot, wb = W.get()
                for fc in range(4):
                    c = j * 4 + fc
                    ps, pb = acc()
                    for kc in range(8):
                        MM(ps[:, 0:T], slot[:, kc * 512 + fc * 128:kc * 512 + fc * 128 + 128], U[:, kc, 0:T], kc == 0, kc == 7, [wb, b_U[kc]], [pb], inc=(kc == 7))
                    ACT(SZ[:, c, 0:T], ps[:, 0:T], AF.Silu, [pb], sz_b)
            COPY("pool", XP16[:, :, 0:3], XPC[:, i, :, :], [b_XPC[i]], b_XP)
            for j in range(3):
                slot, wb = W.get()
                for fc in range(4 if j < 2 else 2):
                    c = j * 4 + fc
                    ps, pb = acc()
                    for kc in range(8):
                        MM(ps[:, 0:T], slot[:, kc * 512 + fc * 128:kc * 512 + fc * 128 + 128], U[:, kc, 0:T], kc == 0, kc == 7, [wb, b_U[kc]], [pb], inc=(kc == 7))
                    TT("dve", XP16[:, c, 3:3 + T], ps[:, 0:T], SDT[:, 26:26 + T], ALU.mult, [pb, b_SDT], [b_XP[c]])
            cwb = VC["e_cw%d" % i]
            for c in range(10):
                ps, pb = acc()
                dg, dgb = dg_batch(VEC[:, cwb + c:cwb + c + 40:10], 4)
                for j in range(4):
                    MM(ps[:, 0:T], dg[:, j, :], XP16[:, c, j:j + T], j == 0, j == 3, [dgb, b_XP[c]], [pb], inc=(j == 3))
                if c < 8:
                    ACT(XS[:, c, 0:T], ps[:, 0:T], AF.Silu, [pb, *CSA], xs_b, bias=vcol("e_cb%d" % i, c), scale=1.0)
                else:
                    ACT(BC16[:, c - 8, 0:T], ps[:, 0:T], AF.Silu, [pb, *CSA], [b_BC[c - 8]], bias=vcol("e_cb%d" % i, c), scale=1.0)
            COPY("pool", XPC[:, i, :, :], XP16[:, :, T:T + 3], b_XP, [b_XPC[i]])
            for c in range(6):
                s = c % 2
                ACT(SQ[:, s, 0:T], QK32[:, c, 0:T], AF.Square, b_QK[2 * c:2 * c + 2], [b_SQ[s]])
                ps, pb = acc()
                MM(ps[:, 0:T], onesblk, SQ[:, s, 0:T], True, True, [*CSA, b_SQ[s]], [pb], inc=True)
                rstd_from(ps, pb, T, 1.0 / 64, EPS, 0)
                if c < 4:
                    STT(Q16[:, c, 0:T], QK32[:, c, 0:T], vcol("e_qn%d" % i, c), RS[:, 0, 0:T], ALU.mult, ALU.mult, b_QK[2 * c:2 * c + 2] + [*CSA, b_RS[0]], [b_Q16[c]])
                else:
                    STT(K16[:, c - 4, 0:T], QK32[:, c, 0:T], vcol("e_kn%d" % i, 0), RS[:, 0, 0:T], ALU.mult, ALU.mult, b_QK[2 * c:2 * c + 2] + [*CSA, b_RS[0]], [b_K16[c - 4]])
            for c in range(8):
                ACT(G32[:, c, 0:T], XS[:, c, 0:T], AF.Identity, xs_b + CSA, [b_G[c], b_XT], scale=vcol("e_ds%d" % i, c))
            fmeta = SDT[:, 25:26]
            TT("dve", E16[:, 0, 0:256].rearrange("p (a k) -> p a k", a=2), K16[:, :, 384:512], KMETA[:, i, :, :], ALU.subtract, b_K16 + [b_KMETA[i]], [b_E16[0]])
            STT(KMETA[:, i, :, :], E16[:, 0, 0:256].rearrange("p (a k) -> p a k", a=2), fmeta, KMETA[:, i, :, :], ALU.mult, ALU.add, [b_E16[0], b_SDT, b_KMETA[i]], [b_KMETA[i]])
            TT("dve", E16[:, 1, 0:128], VT[:, 3, :], VMETA[:, i, :], ALU.subtract, [b_VT[3], b_VMETA[i]], [b_E16[1]])
            STT(VMETA[:, i, :], E16[:, 1, 0:128], fmeta, VMETA[:, i, :], ALU.mult, ALU.add, [b_E16[1], b_SDT, b_VMETA[i]], [b_VMETA[i]])
            w0 = PSD[2]
            w1 = PSD[3]
            w0b = b_ps[4:6]
            w1b = b_ps[6:8]
            w0v = w0[:, :].rearrange("p (h q) -> p h q", h=8)
            w1v = w1[:, :].rearrange("p (h q) -> p h q", h=8)
            for b in range(nb if KDBG >= 18 else 0):
                gb = gb0 + b
                cols = slice(b * 128, (b + 1) * 128)
                kbs = [(lambda kv: KMETA[:, i, kv, :], [b_KMETA[i]], VMETA[:, i, :], [b_VMETA[i]], -1)]
                if b == 0:
                    kbs.append((lambda kv: KPREV[:, i, kv, :], [b_KPREV[i]], VPREV[:, i, :], [b_VPREV[i]], 3))
                else:
                    kbs.append((lambda kv, b=b: K16[:, kv, (b - 1) * 128:b * 128], b_K16, VT[:, b - 1, :], [b_VT[b - 1]], 3))
                kbs.append((lambda kv, b=b: K16[:, kv, b * 128:(b + 1) * 128], b_K16, VT[:, b, :], [b_VT[b]], 4))

                def coef(t, b=b):
                    return SDT[:, 4 + b * 5 + t:4 + b * 5 + t + 1]
                TS("dve", TME[:, :], TABv[:, 0, :, :].rearrange("p h q -> p (h q)"), coef(0), None, ALU.mult, None, [b_SDT, *CSA], [b_TME])
                for t in (1, 2):
                    STT(TME[:, :], TABv[:, t, :, :].rearrange("p h q -> p (h q)"), coef(t), TME[:, :], ALU.mult, ALU.add, [b_SDT, b_TME, *CSA], [b_TME])
                ot_ps, ot_pb = acc()
                nk = len(kbs)
                for ki, (kf, kbuf, vap, vbuf, tabi) in enumerate(kbs):
                    s = ki % 2
                    ws, wsb = (w1, w1b) if ki == 1 else (w0, w0b)
                    for h in range(8):
                        kv, c, half = h // 4, h // 2, h % 2
                        hp = half * 4 + c
                        MM(ws[:, hp * 128:(hp + 1) * 128], kf(kv)[half * 64:half * 64 + 64, :], Q16[half * 64:half * 64 + 64, c, cols], True, True,
                           kbuf + [b_Q16[c]], wsb, inc=(h == 7))
                    if KDBG < 19:
                        continue
                    for hb in range(2):
                        ACT(E16[:, s, hb * 512:(hb + 1) * 512], ws[:, hb * 512:(hb + 1) * 512], AF.Exp, wsb, [b_E16[s]], scale=0.125)
                    if KDBG < 20:
                        continue
                    if tabi < 0:
                        TT("dve", PT[:, ki, :], E16[:, s, :], TME[:, :], ALU.mult, [b_E16[s], b_TME], [b_PT[ki]])
                    else:
                        STT(PT[:, ki, :], E16[:, s, :], coef(tabi), TABv[:, tabi, :, :].rearrange("p h q -> p (h q)"), ALU.mult, ALU.mult, [b_E16[s], b_SDT, *CSA], [b_PT[ki]])
                if KDBG < 21:
                    continue
                for hh in range(2):
                    for ki in range(nk):
                        MM(w1[:, hh * 512:(hh + 1) * 512], ones16, PT[:, ki, hh * 512:(hh + 1) * 512], ki == 0, ki == nk - 1, [*CSA, b_PT[ki]], w1b,
                           inc=(hh == 1 and ki == nk - 1))
                if KDBG < 22:
                    continue
                for h in range(8):
                    kv, c, half = h // 4, h // 2, h % 2
                    for ki, (kf, kbuf, vap, vbuf, tabi) in enumerate(kbs):
                        hp = half * 4 + c
                        MM(ot_ps[half * 64:half * 64 + 64, c * 128:(c + 1) * 128], vap[:, kv * 64:kv * 64 + 64], PT[:, ki, hp * 128:(hp + 1) * 128],
                           ki == 0, ki == nk - 1, vbuf + [b_PT[ki]], [ot_pb], inc=(h == 7 and ki == nk - 1), tp=(0, half * 64))
                if KDBG < 23:
                    continue
                RDv = RD[:, :].rearrange("p (c q) -> p c q", c=4)
                for half in range(2):
                    pr = slice(half * 64, half * 64 + 64)
                    TT("dve", RDv[pr, :, :], w1v[pr, half * 4:half * 4 + 4, :],
                       ESK[pr, i, half::2].unsqueeze(2).to_broadcast([64, 4, 128]), ALU.add, w1b + CSA, [b_RD])
                ACT(RD[:, :], RD[:, :], AF.Ln, [b_RD], [b_RD])
                ACT(RD[:, :], RD[:, :], AF.Exp, [b_RD], [b_RD], scale=-1.0)
                TT("dve", MIX[:, 0:4, cols], ot_ps[:, 0:512].rearrange("p (c q) -> p c q", c=4), RDv[:, :, :], ALU.mult, [ot_pb, b_RD], b_QK[0:4])
                if KDBG < 31:
                    continue
                DT = DTALL[:, b, :]
                ADT = SM[:, 0, :]
                NCS = SM[:, 1, :]
                EL = SM[:, 2, :]
                TT("dve", ADT, DT, AROW[:, i, :], ALU.mult, [b_RS[1], *CSA], [b_SM[0]])
                for half in range(2):
                    ps, pb = acc()
                    for j in range(4):
                        c = half * 4 + j
                        TR(ps[:, j * 128:(j + 1) * 128], XS[:, c, cols], ident, CSA + xs_b, [pb], inc=(j == 3))
                    COPY("act", XTM[:, half * 512:(half + 1) * 512], ps[:, 0:512], [pb], [b_XTM])
                st_ps, st_pb = bank(3), b_ps[3]
                P.op("pe", lambda e, cols=cols: e.transpose(bank(3)[:, 256:320].bitcast(BF16), BC16[:, 0, cols], ident16), [*CSA, b_BC[0]], [st_pb])
                COPY("act", BTM[:, :], st_ps[:, 256:320].bitcast(BF16), [st_pb], [b_BTM])
                MM(st_ps[:, 0:16], tri, ADT, True, True, [*CSA, b_SM[0]], [st_pb], inc=False)
                MM(st_ps[:, 16:32], ones32, ADT, True, True, [*CSA, b_SM[0]], [st_pb], inc=True)
                TS("dve", NCS, st_ps[:, 0:16], -1.0, None, ALU.mult, None, [st_pb], [b_SM[1]])
                TT("dve", EL, NCS, st_ps[:, 16:32], ALU.add, [st_pb, b_SM[1]], [b_SM[2]])
                ACT(EL, EL, AF.Exp, [b_SM[2]], [b_SM[2]])
                ACT(DEC2[0:64, :], st_ps[0:64, 16:24], AF.Exp, [st_pb], [b_DEC2])
                ACT(DEC2[64:128, :], st_ps[64:128, 24:32], AF.Exp, [st_pb], [b_DEC2])
                XTMv = XTM[:, :].rearrange("p (h d) -> p h d", h=16)
                TT("dve", XDT[:, :].rearrange("p (h d) -> p h d", h=16), XTMv, DT.unsqueeze(2).to_broadcast([128, 16, 64]), ALU.mult, [b_XTM, b_RS[1]], [b_XDT])
                TT("dve", XW[:, :].rearrange("p (h d) -> p h d", h=16), XDT[:, :].rearrange("p (h d) -> p h d", h=16),
                   EL.unsqueeze(2).to_broadcast([128, 16, 64]), ALU.mult, [b_XDT, b_SM[2]], [b_XW])
                cbs = [acc(), acc()]
                for g in range(2):
                    pr = slice(g * 64, g * 64 + 64)
                    MM(cbs[g][0][:, 0:128], BC16[pr, 0, cols], BC16[pr, 1, cols], True, True, b_BC, [cbs[g][1]], inc=True)
                for j in range(8):
                    MM(w1[0:64, j * 128:(j + 1) * 128], ADT[:, j:j + 1].to_broadcast([128, 64]), tri, True, True, [b_SM[0], *CSA], w1b, inc=False)
                    MM(w1[64:128, j * 128:(j + 1) * 128], ADT[:, 8 + j:9 + j].to_broadcast([128, 64]), tri, True, True, [b_SM[0], *CSA], w1b, inc=(j == 7), tp=(0, 64))
                for hb in range(2):
                    ACT(E16[:, 0, hb * 512:(hb + 1) * 512], w1[:, hb * 512:(hb + 1) * 512], AF.Exp, w1b, [b_E16[0]])
                TT("dve", C2[:, :].rearrange("p (j l) -> p j l", j=8), E16[:, 0, :].rearrange("p (j l) -> p j l", j=8),
                   BC16[:, 1, cols].unsqueeze(1).to_broadcast([128, 8, 128]), ALU.mult, [b_E16[0], b_BC[1]], [b_C2])
                wg = [(w0, w0b), (w1, w1b)]
                for g in range(2):
                    for j in range(8):
                        h = g * 8 + j
                        MM(wg[g][0][:, j * 128:(j + 1) * 128], ADT[:, h:h + 1].to_broadcast([128, 128]), tri, True, False, [b_SM[0], *CSA], wg[g][1], inc=False)
                        MM(wg[g][0][:, j * 128:(j + 1) * 128], ident16, neg16, False, True, CSA, wg[g][1], inc=(j == 7))
                for g in range(2):
                    eb = 1 - g
                    for j in range(8):
                        h = g * 8 + j
                        ACT(E16[:, eb, j * 128:(j + 1) * 128], wg[g][0][:, j * 128:(j + 1) * 128], AF.Exp, wg[g][1] + [b_SM[1]], [b_E16[eb]], bias=NCS[:, h:h + 1], scale=1.0)
                    TT("dve", WT[:, g, :].rearrange("p (j l) -> p j l", j=8), E16[:, eb, :].rearrange("p (j l) -> p j l", j=8),
                       cbs[g][0][:, 0:128].unsqueeze(1).to_broadcast([128, 8, 128]), ALU.mult, [b_E16[eb], cbs[g][1]], [b_WT[g]])
                for c in range(8):
                    g = c // 4
                    for half in range(2):
                        h = 2 * c + half
                        j = h % 8
                        out = w0[half * 64:half * 64 + 64, c * 128:(c + 1) * 128]
                        MM(out, XDT[:, h * 64:(h + 1) * 64], WT[:, g, j * 128:(j + 1) * 128], True, False, [b_XDT, b_WT[g]], w0b, inc=False, tp=(0, half * 64))
                        MM(out, ST16[g * 64:g * 64 + 64, i, j * 64:(j + 1) * 64], C2[g * 64:g * 64 + 64, j * 128:(j + 1) * 128], False, True,
                           [b_ST16[i], b_C2], w0b, inc=(c == 7 and half == 1), tp=(g * 64, half * 64))
                for hb in range(2):
                    TT("dve", G32[:, 4 * hb:4 * hb + 4, cols], w0v[:, 4 * hb:4 * hb + 4, :], G32[:, 4 * hb:4 * hb + 4, cols], ALU.add, w0b + b_G, b_G)
                TT("pool", G32[:, :, cols], G32[:, :, cols], SZ[:, :, cols], ALU.mult, b_G + sz_b, b_G)
                sn_ps, sn_pb = acc()
                for g in range(2):
                    MM(sn_ps[g * 64:g * 64 + 64, 0:512], BTM[:, g * 64:g * 64 + 64], XW[:, g * 512:(g + 1) * 512], True, True, [b_BTM, b_XW], [sn_pb], inc=(g == 1), tp=(0, g * 64))
                STv = ST32[:, i, :].rearrange("p (j d) -> p j d", j=8)
                TT("dve", STv, STv, DEC2[:, :].unsqueeze(2).to_broadcast([128, 8, 64]), ALU.mult, [b_ST32[i], b_DEC2], [b_ST32[i]])
                TT("dve", ST32[:, i, :], ST32[:, i, :], sn_ps[:, 0:512], ALU.add, [b_ST32[i], sn_pb], [b_ST32[i]])
                COPY("act", ST16[:, i, :], ST32[:, i, :], [b_ST32[i]], [b_ST16[i]])
            COPY("pool", KPREV[:, i, :, :], K16[:, :, (nb - 1) * 128:nb * 128], b_K16, [b_KPREV[i]])
            COPY("pool", VPREV[:, i, :], VT[:, nb - 1, :], [b_VT[nb - 1]], [b_VPREV[i]])
            for g in range(2):
                ps, pb = acc()
                for k in range(4):
                    c = g * 4 + k
                    s = c % 2
                    ACT(SQ[:, s, 0:T], G32[:, c, 0:T], AF.Square, [b_G[c]], [b_SQ[s]])
                    MM(ps[:, 0:T], ones32, SQ[:, s, 0:T], k == 0, k == 3, [*CSA, b_SQ[s]], [pb], inc=True)
                rstd_from(ps, pb, T, 1.0 / 512, EPS, 0)
                for k in range(4):
                    c = g * 4 + k
                    STT(MIX[:, 4 + c, 0:T], G32[:, c, 0:T], vcol("e_nw%d" % i, c), RS[:, 0, 0:T], ALU.mult, ALU.mult, [b_G[c], *CSA, b_RS[0]], [b_QK[4 + c]])
            for j in range(4):
                slot, wb = W.get()
                for fc in range(2):
                    oc = j * 2 + fc
                    ps, pb = acc()
                    for kc in range(12):
                        MM(ps[:, 0:T], slot[:, kc * 256 + fc * 128:kc * 256 + fc * 128 + 128], MIX[:, kc, 0:T], kc == 0, kc == 11, [wb, b_QK[kc]], [pb], inc=(kc == 11))
                    TT("dve", H[:, oc, 0:T], H[:, oc, 0:T], ps[:, 0:T], ALU.add, [pb, b_H[oc]], [b_H[oc]])

        MEMSET("pool", XPG[:, :], 0.0, ALLX)
        P.dma("sp", "xdz", lambda e: e.dma_start(out=xdst[0:512, :].rearrange("(b p) f -> p b f", p=128), in_=XT[:, :, :]), [b_XT], [b_DST])
        load_x(0)
        for ti, (b0, nb) in enumerate(tiles):
            T = nb * 128
            load_sdat(ti)
            load_exchange(ti)
            transpose_in(ti)
            even_mixer(T, 0, ti)
            mlp(T, 0)
            if ti + 1 < len(tiles):
                load_x(ti + 1)
            conformer(T, 0, ti)
            mlp(T, 1)
            store_out(ti)
        P.wait_all("sp", [b_yout])
        P.emit()
    return nc


def _piece_k1024(w, cols):
    sel = np.zeros((1024, 512), np.float32)
    sel[:, :len(cols)] = w[:, cols]
    return np.ascontiguousarray(sel.reshape(8, 128, 512).transpose(1, 0, 2).reshape(128, 4096))


def _pack_weights(stage, w_in, w_out, pw1_w, pw2_w, w_up, w_down):
    pieces = []
    r = np.arange

    def mlp_pieces(l):
        for j in range(8):
            pieces.append(_piece_k1024(w_up[l], r(j * 512, (j + 1) * 512)))
        for oc in range(8):
            blk = w_down[l][:, oc * 128:(oc + 1) * 128]
            pieces.append(np.ascontiguousarray(blk.reshape(32, 128, 128).transpose(1, 0, 2).reshape(128, 4096)))

    for l in (2 * stage, 2 * stage + 1):
        i = l // 2
        if l % 2 == 0:
            w = w_in[i]
            pieces.append(_piece_k1024(w, r(0, 512)))
            a1 = np.concatenate([r(512, 576), r(512, 576), r(576, 640), r(576, 640), r(640, 768), r(3072, 3088)])
            pieces.append(_piece_k1024(w, a1))
            pieces.append(_piece_k1024(w, r(768, 1280)))
            pieces.append(_piece_k1024(w, r(1280, 1792)))
            pieces.append(_piece_k1024(w, r(1792, 2304)))
            pieces.append(_piece_k1024(w, r(2304, 2816)))
            pieces.append(_piece_k1024(w, r(2816, 3072)))
            wo = w_out[i]
            for j in range(4):
                blk = wo[:, j * 256:(j + 1) * 256].reshape(12, 128, 256).transpose(1, 0, 2).reshape(128, 3072)
                p = np.zeros((128, 4096), np.float32)
                p[:, :3072] = blk
                pieces.append(p)
        else:
            for j in range(4):
                pieces.append(_piece_k1024(pw1_w[i], r(j * 512, (j + 1) * 512)))
            for j in range(2):
                pieces.append(_piece_k1024(pw2_w[i], r(j * 512, (j + 1) * 512)))
        mlp_pieces(l)
    out = np.stack(pieces, 0)
    assert out.shape[0] == PIECES_PER_TILE, out.shape
    return out


def _fm(v):
    v = np.asarray(v, np.float32)
    return v.reshape(-1, 128).T


def _pack_vecs(inp0, stage):
    inp = {}
    for k, v in inp0.items():
        v = np.asarray(v)
        if k in ("x", "meta_tokens"):
            continue
        if k in ("mlp_norm", "w_up", "w_down"):
            inp[k] = np.concatenate([v[2 * stage:2 * stage + 2], v[2 * stage:2 * stage + 2]], 0) if k == "mlp_norm" else None
        elif k in ("w_in", "w_out", "pw1_w", "pw2_w"):
            inp[k] = None
        else:
            inp[k] = np.stack([v[stage], v[stage]], 0)
    V = np.zeros((128, NV), np.float32)

    def put(name, arr):
        arr = np.asarray(arr, np.float32)
        V[:, VC[name]:VC[name] + arr.shape[1]] = arr

    for i in range(2):
        put("e_g%d" % i, _fm(inp["mix_norm_even"][i]))
        cw = inp["ssm_conv_w"][i]
        put("e_cw%d" % i, np.concatenate([_fm(cw[j]) for j in range(4)], 1))
        put("e_cb%d" % i, _fm(inp["ssm_conv_b"][i]))
        put("e_ds%d" % i, _fm(np.repeat(inp["d_skip"][i], 64)))
        put("e_nw%d" % i, _fm(inp["ssm_norm_w"][i]))
        put("e_qn%d" % i, _fm(np.tile(inp["q_norm"][i], 8)))
        put("e_kn%d" % i, _fm(np.tile(inp["k_norm"][i], 2)))
        put("o_g%d" % i, _fm(inp["mix_norm_odd"][i]))
        put("o_b1%d" % i, _fm(inp["pw1_b"][i]))
        dw = inp["dw_w"][i]
        put("o_dw%d" % i, np.concatenate([_fm(dw[j]) for j in range(31)], 1))
        put("o_db%d" % i, _fm(inp["dw_b"][i]))
        put("o_lg%d" % i, _fm(inp["ln_g"][i]))
        put("o_lb%d" % i, _fm(inp["ln_b"][i]))
        put("o_b2%d" % i, _fm(inp["pw2_b"][i]))
    for l in range(4):
        put("m_g%d" % l, _fm(inp["mlp_norm"][l]))
    R = np.zeros((128, NR), np.float32)
    for i in range(2):
        R[:, i * 40:i * 40 + 16] = np.asarray(inp["dt_bias"][i], np.float32)[None, :]
        R[:, i * 40 + 16:i * 40 + 32] = np.asarray(inp["a_log"][i], np.float32)[None, :]
        R[:, i * 40 + 32:i * 40 + 40] = np.asarray(inp["sinks"][i], np.float32)[None, :]
    return V, R


def _consts():
    C = np.zeros((128, NCST), np.float32)
    C[:, 0:128] = np.eye(128, dtype=np.float32)
    s = np.arange(128)[:, None]
    l = np.arange(128)[None, :]
    C[:, 128:256] = (s <= l)
    C[:, 256:384] = 1.0
    C[:, 384:512] = ((s // 64) == (l // 64))
    C[:, 512:640] = np.where(l < s, -30000.0, 0.0)
    slopes = np.exp2(-8.0 * np.arange(1, 9, dtype=np.float64) / 8)
    k = np.arange(128)[:, None].astype(np.float64)
    q = np.arange(128)[None, :].astype(np.float64)
    T = np.zeros((128, 5, 8, 128), np.float64)
    for hp in range(8):
        h = hp
        sl = slopes[2 * (hp % 4) + hp // 4]
        valid = (k >= 112) & (q >= 112) & (k <= q)
        T[:, 0, h, :] = np.where(valid, np.exp(-sl * (q - k)), 0.0)
        valid = (k >= 112) & (q >= 0)
        T[:, 1, h, :] = np.where(valid, np.exp(-sl * np.minimum(q + 16 - (k - 112), 128)), 0.0)
        T[:, 2, h, :] = np.where(k >= 112, np.exp(-sl * 128.0), 0.0)
        T[:, 3, h, :] = np.where(k > q, np.exp(-sl * (q + 128 - k)), 0.0)
        T[:, 4, h, :] = np.where(k <= q, np.exp(-sl * (q - k)), 0.0)
    return C, T.reshape(128, NTAB).astype(np.float32)


_NC_CACHE = {}


def _step_data(nsteps, lag=0):
    S = np.zeros((nsteps, 128, SD), np.float32)
    p = np.arange(128)
    for s_ in range(nsteps):
        t = s_ - lag
        for b in range(4):
            gb = -3 + 4 * t + b
            if t < 0 or gb < 0:
                valid = np.zeros(128, np.float32)
                co = [0, 0, 0, 0, 0]
            elif gb == 0:
                valid = (p >= 112).astype(np.float32)
                co = [1, 0, 0, 0, 0]
            elif gb == 1:
                valid = np.ones(128, np.float32)
                co = [0, 1, 0, 0, 1]
            else:
                valid = np.ones(128, np.float32)
                co = [0, 0, 1, 1, 1]
            S[s_, :, b] = valid
            S[s_, :, 4 + b * 5:4 + b * 5 + 5] = np.asarray(co, np.float32)[None, :]
            S[s_, :, 26 + b * 128:26 + (b + 1) * 128] = valid[None, :]
        S[s_, :, 24] = 1.0 if lag > 0 else 0.0
        S[s_, :, 25] = 1.0 if t == 0 else 0.0
    return S


def _run(inputs, nsteps, nbatch, nlayers=4):
    key = nsteps
    if key not in _NC_CACHE:
        _NC_CACHE[key] = build(nsteps)
    nc = _NC_CACHE[key]
    x = np.asarray(inputs["x"], np.float32)
    meta = np.asarray(inputs["meta_tokens"], np.float32)
    C, TB = _consts()
    in_maps = []
    per_stage = []
    for stage in range(2):
        wall = _pack_weights(stage, *[np.asarray(inputs[k], np.float32) for k in ("w_in", "w_out", "pw1_w", "pw2_w", "w_up", "w_down")])
        V, R = _pack_vecs(inputs, stage)
        per_stage.append((wall, V, R, _step_data(nsteps, stage)))
    zeros = np.zeros((nsteps * 512, D), np.float32)
    for b in range(nbatch):
        xin = np.zeros((nsteps * 512, D), np.float32)
        xin[3 * 128 + 112:4 * 128] = meta
        xin[4 * 128:4 * 128 + x.shape[1]] = x[b]
        for stage in range(2):
            wall, V, R, SDA = per_stage[stage]
            in_maps.append({"xin": xin if stage == 0 else zeros, "wall": wall, "vecs": V, "rows": R, "cst": C, "tab": TB, "sdat": SDA})
    res = run_bass_kernel_spmd(nc, in_maps, core_ids=list(range(2 * nbatch)))
    return np.stack([res.results[2 * b + 1]["y"][8 * 128:8 * 128 + x.shape[1]] for b in range(nbatch)], 0)


def kernel(**inputs):
    x = inputs["x"]
    bsz, seq, _ = x.shape
    nsteps = (seq // 128 + 4) // 4 + 1
    return _run(inputs, nsteps, bsz).astype(np.float32)
```

```python
import contextlib
import os
import numpy as np
import concourse.bass as bass
import concourse.mybir as mybir
from concourse.bass_utils import run_bass_kernel_spmd

F32 = mybir.dt.float32
BF16 = mybir.dt.bfloat16
AF = mybir.ActivationFunctionType
ALU = mybir.AluOpType

D = 1024
RGROUPS = [[0, 1], [2, 3], [4, 5], [6, 7]]
KDBG = int(os.environ.get('KDBG', '99'))
NSLOT = 4
EPS = 1e-6
LN_EPS = 1e-5


class Buf:
    __slots__ = ("name", "w", "r")

    def __init__(self, name=""):
        self.name = name
        self.w = None
        self.r = {}


class Prog:
    ENGS = ("pe", "act", "dve", "pool", "sp")

    def __init__(self, nc):
        self.nc = nc
        self.q = {e: [] for e in self.ENGS}
        self.cnt = {e: 0 for e in self.ENGS}
        self.waited = {e: {} for e in self.ENGS}
        self.dma_cnt = {}
        self.sem_keys = set(self.ENGS)

    def _deps(self, eng, reads, writes):
        evs = []
        for b in reads:
            if b.w is not None:
                evs.append(("raw", b.w))
        for b in writes:
            if b.w is not None:
                evs.append(("waw", b.w))
            for ev in b.r.values():
                evs.append(("war", ev))
        waits = {}
        for kind, (weng, key, val) in evs:
            if weng == eng:
                if val > self.cnt[eng]:
                    continue
            if self.waited[eng].get(key, 0) >= val:
                continue
            if waits.get(key, 0) < val:
                waits[key] = val
        for key, val in waits.items():
            self.waited[eng][key] = val
        return list(waits.items())

    def _commit(self, ev, ekey, reads, writes):
        for b in writes:
            b.w = ev
            b.r = {}
        for b in reads:
            b.r[ekey] = ev

    def op(self, eng, fn, reads=(), writes=(), inc=True):
        waits = self._deps(eng, reads, writes)
        if inc:
            self.cnt[eng] += 1
            val = self.cnt[eng]
        else:
            val = self.cnt[eng] + 1
        ev = (eng, eng, val)
        self.q[eng].append((waits, fn, (eng, 1) if inc else None))
        self._commit(ev, eng, reads, writes)
        return ev

    def dma(self, eng, sem_key, fn, reads=(), writes=()):
        self.sem_keys.add(sem_key)
        waits = self._deps(eng, reads, writes)
        n = self.dma_cnt.get(sem_key, 0) + 1
        self.dma_cnt[sem_key] = n
        ev = (None, sem_key, 16 * n)
        self.q[eng].append((waits, fn, (sem_key, 16)))
        self._commit(ev, ("dma", sem_key), reads, writes)
        return ev

    def coll(self, eng, sem_key, fn, reads=(), writes=()):
        self.sem_keys.add(sem_key)
        waits = self._deps(eng, reads, writes)
        n = self.dma_cnt.get(sem_key, 0) + 1
        self.dma_cnt[sem_key] = n
        ev = (None, sem_key, n)
        self.q[eng].append((waits, fn, (sem_key, 1)))
        self._commit(ev, ("dma", sem_key), reads, writes)
        return ev

    def wait_all(self, eng, bufs):
        waits = self._deps(eng, bufs, ())
        self.q[eng].append((waits, None, None))

    def emit(self):
        nc = self.nc
        keys = sorted(self.sem_keys, key=str)
        with contextlib.ExitStack() as st:
            sems = {}
            for i, k in enumerate(keys):
                sems[k] = st.enter_context(nc.semaphore("sm%d" % i))
            block = st.enter_context(nc.Block())
            handles = {"pe": block.tensor, "act": block.scalar, "dve": block.vector,
                       "pool": block.gpsimd, "sp": block.sync}

            def make(e):
                def body(engobj):
                    for waits, fn, inc in self.q[e]:
                        for key, val in waits:
                            engobj.wait_ge(sems[key], val)
                        if fn is None:
                            continue
                        ins = fn(engobj)
                        if inc is not None:
                            ins.then_inc(sems[inc[0]], inc[1])
                return body

            for e in self.ENGS:
                if self.q[e]:
                    handles[e](make(e))


def _vec_layout():
    vc = {}
    n = 0

    def add(name, k):
        nonlocal n
        vc[name] = n
        n += k

    for i in range(2):
        add("e_g%d" % i, 8)
        add("e_cw%d" % i, 40)
        add("e_cb%d" % i, 10)
        add("e_ds%d" % i, 8)
        add("e_nw%d" % i, 8)
        add("e_qn%d" % i, 4)
        add("e_kn%d" % i, 1)
    for i in range(2):
        add("o_g%d" % i, 8)
        add("o_b1%d" % i, 16)
        add("o_dw%d" % i, 31 * 8)
        add("o_db%d" % i, 8)
        add("o_lg%d" % i, 8)
        add("o_lb%d" % i, 8)
        add("o_b2%d" % i, 8)
    for l in range(4):
        add("m_g%d" % l, 8)
    return vc, n


VC, NV = _vec_layout()
NR = 2 * 40
NCST = 5 * 128
NTAB = 5 * 8 * 128
SD = 4 + 20 + 2 + 512
PIECES_PER_TILE = 27 + 22


def build(nsteps, nlayers=4):
    nc = bass.Bass("TRN2", target_bir_lowering=False)
    tiles = [(-3 + 4 * s_, 4) for s_ in range(nsteps)]
    nblk_seq = 4 * nsteps

    xin = nc.dram_tensor("xin", [nblk_seq * 128, D], F32, kind="ExternalInput").ap()
    sdat_d = nc.dram_tensor("sdat", [nsteps, 128, SD], F32, kind="ExternalInput").ap()
    wall = nc.dram_tensor("wall", [PIECES_PER_TILE, 128, 4096], F32, kind="ExternalInput").ap()
    vecs_d = nc.dram_tensor("vecs", [128, NV], F32, kind="ExternalInput").ap()
    rows_d = nc.dram_tensor("rows", [128, NR], F32, kind="ExternalInput").ap()
    cst_d = nc.dram_tensor("cst", [128, NCST], F32, kind="ExternalInput").ap()
    tab_d = nc.dram_tensor("tab", [128, NTAB], F32, kind="ExternalInput").ap()
    yout = nc.dram_tensor("y", [nblk_seq * 128, D], F32, kind="ExternalOutput").ap()

    xsrc = nc.dram_tensor("xsrc", [512, D], F32).ap()
    xdst = nc.dram_tensor("xdst", [1024, D], F32).ap()

    P = Prog(nc)
    st = contextlib.ExitStack()
    with st:
        def sb(name, shape, dt):
            return st.enter_context(nc.sbuf_tensor(name, shape, dt))

        H = sb("H", [128, 8, 512], F32)
        BIGF = sb("BIGF", [128, 8192], F32)
        XPG = sb("XPG", [128, 4096], F32)
        QK32 = sb("QK32", [128, 6, 512], F32)
        U = sb("U", [128, 8, 512], BF16)
        ring = sb("ring", [128, NSLOT, 4096], BF16)
        Q16 = sb("Q16", [128, 4, 512], BF16)
        K16 = sb("K16", [128, 2, 512], BF16)
        VT = sb("VT", [128, 4, 128], BF16)
        BC16 = sb("BC16", [128, 2, 512], BF16)
        XTM = sb("XTM", [128, 1024], BF16)
        BTM = sb("BTM", [128, 128], BF16)
        XDT = sb("XDT", [128, 1024], BF16)
        XW = sb("XW", [128, 1024], BF16)
        E16 = sb("E16", [128, 2, 1024], BF16)
        PT = sb("PT", [128, 3, 1024], BF16)
        WT = sb("WT", [128, 2, 1024], BF16)
        C2 = sb("C2", [128, 1024], BF16)
        RD = sb("RD", [128, 512], F32)
        SM = sb("SM", [128, 8, 16], F32)
        DEC2 = sb("DEC2", [128, 8], F32)
        SQ = sb("SQ", [128, 4, 512], BF16)
        RS = sb("RS", [128, 2, 512], F32)
        TMPF = sb("TMPF", [128, 2, 512], F32)
        KPREV = sb("KPREV", [128, 2, 2, 128], BF16)
        KMETA = sb("KMETA", [128, 2, 2, 128], BF16)
        VPREV = sb("VPREV", [128, 2, 128], BF16)
        VMETA = sb("VMETA", [128, 2, 128], BF16)
        XPC = sb("XPC", [128, 2, 10, 3], BF16)
        XP16 = sb("XP16", [128, 10, 515], BF16)
        ST32 = sb("ST32", [128, 2, 512], F32)
        ST16 = sb("ST16", [128, 2, 512], BF16)
        AHC = sb("AHC", [128, 2, 8, 30], BF16)
        DGR = sb("DGR", [128, 2, 8, 128], BF16)
        CST = sb("CST", [128, NCST], F32)
        TAB = sb("TAB", [128, NTAB], BF16)
        C16 = sb("C16", [128, 4, 128], BF16)
        VEC = sb("VEC", [128, NV], F32)
        ROWS = sb("ROWS", [128, NR], F32)
        AROW = sb("AROW", [128, 2, 16], F32)
        ESK = sb("ESK", [128, 2, 8], F32)
        SDT = sb("SDT", [128, SD], F32)
        TME = sb("TME", [128, 1024], BF16)

        PSD = [st.enter_context(nc.psum_tensor("psd%d" % i, [128, 1024], F32)) for i in range(4)]
        b_ps = [Buf("ps%d" % i) for i in range(8)]

        def bank(i):
            return PSD[i // 2][:, (i % 2) * 512:(i % 2) * 512 + 512]

        HID = BIGF[:, 0:8192].bitcast(BF16).rearrange("p (c t) -> p c t", c=32)
        SZ = BIGF[:, 0:4096].rearrange("p (c t) -> p c t", c=8)
        XS = BIGF[:, 4096:8192].rearrange("p (c t) -> p c t", c=8)
        AH16 = BIGF[:, 0:2168].bitcast(BF16).rearrange("p (c t) -> p c t", c=8)
        CO = BIGF[:, 4096:8192].rearrange("p (c t) -> p c t", c=8)
        G32 = XPG[:, 0:4096].rearrange("p (c t) -> p c t", c=8)
        XT = XPG[:, 0:4096].rearrange("p (b f) -> p b f", b=4)
        MIX = QK32[:, :, :].rearrange("p c t -> p (c t)").bitcast(BF16).rearrange("p (c t) -> p c t", c=12)
        OUTST = U[:, :, :].rearrange("p c t -> p (c t)").bitcast(F32).rearrange("p (s f) -> p s f", s=2)
        ident = CST[:, 0:128]
        tri = CST[:, 128:256]
        ones32 = CST[:, 256:384]
        onesblk = CST[:, 384:512]
        ones16 = C16[:, 0, :]
        neg16 = C16[:, 1, :]
        ident16 = C16[:, 2, :]
        onesblk16 = C16[:, 3, :]
        TABv = TAB[:, :].rearrange("p (a h q) -> p a h q", a=5, h=8)

        b_H = [Buf("H%d" % c) for c in range(8)]
        b_U = [Buf("U%d" % c) for c in range(8)]
        b_BIG = [Buf("BIG%d" % c) for c in range(32)]
        b_XP = [Buf("XP%d" % c) for c in range(10)]
        b_G = [Buf("G%d" % c) for c in range(8)]
        b_XT = Buf("XT")
        ALLX = b_G + [b_XT]
        b_QK = [Buf("QK%d" % c) for c in range(12)]
        b_Q16 = [Buf() for _ in range(4)]
        b_K16 = [Buf() for _ in range(2)]
        b_VT = [Buf() for _ in range(4)]
        b_BC = [Buf() for _ in range(2)]
        b_XTM, b_BTM, b_XDT, b_XW, b_C2, b_RD, b_DEC2 = (Buf() for _ in range(7))
        b_E16 = [Buf(), Buf()]
        b_PT = [Buf(), Buf(), Buf()]
        b_WT = [Buf(), Buf()]
        b_SM = [Buf() for _ in range(8)]
        b_SQ = [Buf(), Buf(), Buf(), Buf()]
        b_RS = [Buf(), Buf()]
        b_TMPF = [Buf(), Buf()]
        b_KPREV, b_KMETA, b_VPREV, b_VMETA = ([Buf(), Buf()] for _ in range(4))
        b_XPC = [Buf(), Buf()]
        b_ST32 = [Buf(), Buf()]
        b_ST16 = [Buf(), Buf()]
        b_AHC = [Buf(), Buf()]
        b_c0, b_c1, b_c2 = Buf("c0"), Buf("c1"), Buf("c2")
        CSA = [b_c0, b_c1, b_c2]
        b_yout = Buf("yout")
        b_SDT = Buf("sdt")
        b_SRC = Buf("xsrc")
        b_DST = Buf("xdst")
        b_TME = Buf("tme")

        def bigs(lo, hi):
            return b_BIG[lo // 256:(hi + 255) // 256]

        def ACT(out, in_, func, reads, writes, **kw):
            return P.op("act", lambda e: e.activation(out=out, in_=in_, func=func, **kw), reads, writes)

        def TT(eng, out, in0, in1, op, reads, writes):
            return P.op(eng, lambda e: e.tensor_tensor(out=out, in0=in0, in1=in1, op=op), reads, writes)

        def TS(eng, out, in0, s1, s2, op0, op1, reads, writes):
            if s2 is None:
                return P.op(eng, lambda e: e.tensor_scalar(out=out, in0=in0, scalar1=s1, scalar2=None, op0=op0), reads, writes)
            return P.op(eng, lambda e: e.tensor_scalar(out=out, in0=in0, scalar1=s1, scalar2=s2, op0=op0, op1=op1), reads, writes)

        def STT(out, in0, scalar, in1, op0, op1, reads, writes):
            return P.op("dve", lambda e: e.scalar_tensor_tensor(out=out, in0=in0, scalar=scalar, in1=in1, op0=op0, op1=op1), reads, writes)

        def COPY(eng, out, in_, reads, writes):
            if eng == "act":
                return P.op("act", lambda e: e.activation(out=out, in_=in_, func=AF.Copy), reads, writes)
            return P.op(eng, lambda e: e.tensor_copy(out=out, in_=in_), reads, writes)

        def MM(out, lhsT, rhs, start, stop, reads, writes, inc, tp=None):
            if tp is None:
                return P.op("pe", lambda e: e.matmul(out, lhsT=lhsT, rhs=rhs, start=start, stop=stop), reads, writes, inc=inc)
            return P.op("pe", lambda e: e.matmul(out, lhsT=lhsT, rhs=rhs, start=start, stop=stop, tile_position=tp), reads, writes, inc=inc)

        def TR(out, in_, idm, reads, writes, inc):
            return P.op("pe", lambda e: e.transpose(out, in_, idm), reads, writes, inc=inc)

        def MEMSET(eng, ap, val, writes):
            return P.op(eng, lambda e: e.memset(ap, val), (), writes)

        b_DG = [Buf("dg0"), Buf("dg1")]
        dg_rr = [0]

        def dg_batch(wtaps, n):
            k = dg_rr[0] % 2
            dg_rr[0] += 1
            TT("dve", DGR[:, k, 0:n, :], C16[:, 2:3, :].to_broadcast([128, n, 128]), wtaps.unsqueeze(2).to_broadcast([128, n, 128]), ALU.mult,
               CSA, [b_DG[k]])
            return DGR[:, k, :, :], b_DG[k]

        acc_rr = [0]

        def acc():
            i = acc_rr[0] % 3
            acc_rr[0] += 1
            return bank(i), b_ps[i]

        class WStream:
            def __init__(self, total):
                self.total = total
                self.issued = 0
                self.next = 0
                self.bufs = [Buf("ring%d" % i) for i in range(NSLOT)]

            def _issue(self):
                j = self.issued
                s = j % NSLOT
                src = wall[j % PIECES_PER_TILE]
                P.dma("pool", ("w", s), lambda e: e.dma_start(out=ring[:, s, :], in_=src), (), [self.bufs[s]])
                self.issued += 1

            def get(self):
                j = self.next
                self.next += 1
                while self.issued < min(self.total, j + NSLOT):
                    self._issue()
                return ring[:, j % NSLOT, :], self.bufs[j % NSLOT]

            def skip(self, n):
                for _ in range(n):
                    self.get()

        W = WStream(PIECES_PER_TILE * len(tiles))

        P.dma("sp", "c0", lambda e: e.dma_start(out=CST[:, :], in_=cst_d[:, :]), (), [b_c0])
        P.dma("sp", "c0", lambda e: e.dma_start(out=VEC[:, :], in_=vecs_d[:, :]), (), [b_c0])
        P.dma("sp", "c0", lambda e: e.dma_start(out=ROWS[:, :], in_=rows_d[:, :]), (), [b_c0])
        P.dma("pool", "c1", lambda e: e.dma_start(out=TAB[:, :], in_=tab_d[:, :]), (), [b_c1])
        P.dma("pool", "c1", lambda e: e.dma_start(out=C16[:, 0, :], in_=cst_d[:, 256:384]), (), [b_c1])
        P.dma("pool", "c1", lambda e: e.dma_start(out=C16[:, 1, :], in_=cst_d[:, 512:640]), (), [b_c1])
        P.dma("pool", "c1", lambda e: e.dma_start(out=C16[:, 2, :], in_=cst_d[:, 0:128]), (), [b_c1])
        P.dma("pool", "c1", lambda e: e.dma_start(out=C16[:, 3, :], in_=cst_d[:, 384:512]), (), [b_c1])
        for i in range(2):
            ACT(AROW[:, i, :], ROWS[:, i * 40 + 16:i * 40 + 32], AF.Exp, [b_c0], [b_c2])
            ACT(ESK[:, i, :], ROWS[:, i * 40 + 32:i * 40 + 40], AF.Exp, [b_c0], [b_c2])
        TS("dve", AROW[:, :, :], AROW[:, :, :], -1.0, None, ALU.mult, None, [b_c2], [b_c2])
        MEMSET("pool", ST32[:, :, :], 0.0, b_ST32)
        MEMSET("pool", ST16[:, :, :], 0.0, b_ST16)
        MEMSET("pool", XPC[:, :, :, :], 0.0, b_XPC)
        MEMSET("pool", AHC[:, :, :, :], 0.0, b_AHC)
        MEMSET("pool", KMETA[:, :, :, :], 0.0, b_KMETA)
        MEMSET("pool", VMETA[:, :, :], 0.0, b_VMETA)
        MEMSET("pool", KPREV[:, :, :, :], 0.0, b_KPREV)
        MEMSET("pool", VPREV[:, :, :], 0.0, b_VPREV)

        def vcol(name, c):
            k = VC[name] + c
            return VEC[:, k:k + 1]

        def load_x(ti):
            b0, nb = tiles[ti]
            src = xin[ti * 512:(ti + 1) * 512, :].rearrange("(b p) f -> p b f", p=128)
            P.dma("sp", "xin", lambda e: e.dma_start(out=XT[:, 0:nb, :], in_=src), (), ALLX)

        XT2 = BIGF[:, 0:4096].rearrange("p (b f) -> p b f", b=4)

        def load_exchange(ti):
            src = xdst[0:512, :].rearrange("(b p) f -> p b f", p=128)
            P.dma("sp", "xdl", lambda e: e.dma_start(out=XT2[:, :, :], in_=src), [b_DST], bigs(0, 4096))
            for blk in range(4):
                STT(XT[:, blk, :], XT2[:, blk, :], SDT[:, 24:25], XT[:, blk, :], ALU.mult, ALU.add, bigs(0, 4096) + [b_SDT, b_XT], [b_XT])

        def load_sdat(ti):
            P.dma("sp", "sdat", lambda e: e.dma_start(out=SDT[:, :], in_=sdat_d[ti]), (), [b_SDT])

        def transpose_in(ti):
            b0, nb = tiles[ti]
            k = 0
            for blk in range(nb):
                for half in range(2):
                    ps, pb = acc()
                    for j in range(4):
                        c = half * 4 + j
                        TR(ps[:, j * 128:(j + 1) * 128], XT[:, blk, c * 128:(c + 1) * 128], ident, CSA + [b_XT], [pb], inc=(j == 3))
                    COPY("act" if k % 2 == 0 else "dve", H[:, half * 4:half * 4 + 4, blk * 128:(blk + 1) * 128],
                         ps[:, 0:512].rearrange("p (c t) -> p c t", c=4), [pb], b_H[half * 4:half * 4 + 4])
                    k += 1

        def store_out(ti):
            b0, nb = tiles[ti]
            k = 0
            for blk in range(nb):
                gb = b0 + 3 + blk
                slot = k % 2
                for half in range(2):
                    ps, pb = acc()
                    for j in range(4):
                        c = half * 4 + j
                        TR(ps[:, j * 128:(j + 1) * 128], H[:, c, blk * 128:(blk + 1) * 128], ident, [*CSA, b_H[c]], [pb], inc=(j == 3))
                    COPY("act" if half == 0 else "dve", OUTST[:, slot, half * 512:(half + 1) * 512], ps[:, 0:512], [pb], b_U[slot * 4:slot * 4 + 4])
                dst = yout[gb * 128:(gb + 1) * 128, :]
                P.dma("sp", ("yo", slot), (lambda e, dst=dst, slot=slot: e.dma_start(out=dst, in_=OUTST[:, slot, :])), b_U[slot * 4:slot * 4 + 4], [b_yout])
                dst2 = xsrc[blk * 128:(blk + 1) * 128, :]
                P.dma("sp", ("ys", slot), (lambda e, dst2=dst2, slot=slot: e.dma_start(out=dst2, in_=OUTST[:, slot, :])), b_U[slot * 4:slot * 4 + 4], [b_SRC])
                k += 1
            P.wait_all("pool", [b_SRC])
            P.coll("pool", "cc", lambda e: e.collective_compute("AllGather", ALU.bypass, replica_groups=RGROUPS, ins=[xsrc.opt()], outs=[xdst.opt()]),
                   [b_SRC], [b_DST])

        def rstd_from(ps, pb, T, inv_n, eps, slot, dst=None):
            ap, bf = (RS[:, slot, 0:T], b_RS[slot]) if dst is None else dst
            ACT(ap, ps[:, 0:T], AF.Ln, [pb], [bf], bias=eps, scale=inv_n)
            ACT(ap, ap, AF.Exp, [bf], [bf], scale=-0.5)
            return ap, bf

        def rmsnorm(T, gname):
            ps, pb = bank(3), b_ps[3]
            for c in range(8):
                s = c % 4
                ACT(SQ[:, s, 0:T], H[:, c, 0:T], AF.Square, [b_H[c]], [b_SQ[s]])
                MM(ps[:, 0:T], ones16, SQ[:, s, 0:T], c == 0, c == 7, [*CSA, b_SQ[s]], [pb], inc=True)
            rstd_from(ps, pb, T, 1.0 / D, EPS, 0)
            for c in range(8):
                STT(U[:, c, 0:T], H[:, c, 0:T], vcol(gname, c), RS[:, 0, 0:T], ALU.mult, ALU.mult, [b_H[c], *CSA, b_RS[0]], [b_U[c]])

        def mlp(T, l):
            rmsnorm(T, "m_g%d" % l)
            for j in range(8):
                slot, wb = W.get()
                for fc in range(4):
                    cf = j * 4 + fc
                    ps, pb = acc()
                    for kc in range(8):
                        MM(ps[:, 0:T], slot[:, kc * 512 + fc * 128:kc * 512 + fc * 128 + 128], U[:, kc, 0:T], kc == 0, kc == 7,
                           [wb, b_U[kc]], [pb], inc=(kc == 7))
                    s = cf % 2
                    ACT(TMPF[:, s, 0:T], ps[:, 0:T], AF.Relu, [pb], [b_TMPF[s]])
                    TT("dve", HID[:, cf, 0:T], TMPF[:, s, 0:T], TMPF[:, s, 0:T], ALU.mult, [b_TMPF[s]], [b_BIG[cf]])
            for oc in range(8):
                slot, wb = W.get()
                ps, pb = acc()
                for kc in range(32):
                    MM(ps[:, 0:T], slot[:, kc * 128:(kc + 1) * 128], HID[:, kc, 0:T], kc == 0, kc == 31, [wb, b_BIG[kc]], [pb], inc=(kc == 31))
                TT("dve", H[:, oc, 0:T], H[:, oc, 0:T], ps[:, 0:T], ALU.add, [pb, b_H[oc]], [b_H[oc]])

        def conformer(T, i, ti):
            gb0 = tiles[ti][0]
            rmsnorm(T, "o_g%d" % i)
            ah_b = bigs(0, 2176)
            co_b = bigs(4096, 8192)
            COPY("pool", AH16[:, :, 0:30], AHC[:, i, :, :], [b_AHC[i]], ah_b)
            for j in range(4):
                slot, wb = W.get()
                for fc in range(4):
                    c = (j % 2) * 4 + fc
                    ps, pb = acc()
                    for kc in range(8):
                        MM(ps[:, 0:T], slot[:, kc * 512 + fc * 128:kc * 512 + fc * 128 + 128], U[:, kc, 0:T], kc == 0, kc == 7,
                           [wb, b_U[kc]], [pb], inc=(kc == 7))
                    if j < 2:
                        STT(CO[:, c, 0:T], ps[:, 0:T], vcol("o_b1%d" % i, c), SDT[:, 26:26 + T], ALU.add, ALU.mult, [pb, *CSA, b_SDT], co_b)
                    else:
                        s = c % 2
                        ACT(TMPF[:, s, 0:T], ps[:, 0:T], AF.Sigmoid, [pb, *CSA], [b_TMPF[s]], bias=vcol("o_b1%d" % i, 8 + c), scale=1.0)
                        TT("dve", AH16[:, c, 30:30 + T], CO[:, c, 0:T], TMPF[:, s, 0:T], ALU.mult, [b_TMPF[s]] + co_b, ah_b)
            dwb = VC["o_dw%d" % i]
            for c in range(8):
                ps, pb = acc()
                for j0 in range(0, 31, 8):
                    n = min(8, 31 - j0)
                    dg, dgb = dg_batch(VEC[:, dwb + c + 8 * j0:dwb + c + 8 * (j0 + n):8], n)
                    for jj in range(n):
                        j = j0 + jj
                        MM(ps[:, 0:T], dg[:, jj, :], AH16[:, c, j:j + T], j == 0, j == 30, [dgb] + ah_b, [pb], inc=(jj == n - 1))
                ACT(CO[:, c, 0:T], ps[:, 0:T], AF.Identity, [pb, *CSA], co_b, bias=vcol("o_db%d" % i, c), scale=1.0)
            COPY("pool", AHC[:, i, :, :], AH16[:, :, T:T + 30], ah_b, [b_AHC[i]])
            ps1, pb1 = bank(3), b_ps[3]
            ps2, pb2 = acc()
            for c in range(8):
                s = c % 4
                MM(ps1[:, 0:T], ones32, CO[:, c, 0:T], c == 0, c == 7, CSA + co_b, [pb1], inc=True)
                ACT(SQ[:, s, 0:T], CO[:, c, 0:T], AF.Square, co_b, [b_SQ[s]])
                MM(ps2[:, 0:T], ones16, SQ[:, s, 0:T], c == 0, c == 7, [*CSA, b_SQ[s]], [pb2], inc=True)
            TS("dve", RS[:, 1, 0:T], ps1[:, 0:T], 1.0 / D, None, ALU.mult, None, [pb1], [b_RS[1]])
            TT("dve", TMPF[:, 0, 0:T], RS[:, 1, 0:T], RS[:, 1, 0:T], ALU.mult, [b_RS[1]], [b_TMPF[0]])
            STT(TMPF[:, 1, 0:T], ps2[:, 0:T], 1.0 / D, TMPF[:, 0, 0:T], ALU.mult, ALU.subtract, [pb2, b_TMPF[0]], [b_TMPF[1]])
            ACT(RS[:, 0, 0:T], TMPF[:, 1, 0:T], AF.Ln, [b_TMPF[1]], [b_RS[0]], bias=LN_EPS, scale=1.0)
            ACT(RS[:, 0, 0:T], RS[:, 0, 0:T], AF.Exp, [b_RS[0]], [b_RS[0]], scale=-0.5)
            for c in range(8):
                s = c % 2
                TT("dve", TMPF[:, s, 0:T], CO[:, c, 0:T], RS[:, 1, 0:T], ALU.subtract, co_b + [b_RS[1]], [b_TMPF[s]])
                TT("pool", TMPF[:, s, 0:T], TMPF[:, s, 0:T], RS[:, 0, 0:T], ALU.mult, [b_TMPF[s], b_RS[0]], [b_TMPF[s]])
                ACT(U[:, c, 0:T], TMPF[:, s, 0:T], AF.Silu, [b_TMPF[s], *CSA], [b_U[c]], bias=vcol("o_lb%d" % i, c), scale=vcol("o_lg%d" % i, c))
            for j in range(2):
                slot, wb = W.get()
                for fc in range(4):
                    oc = j * 4 + fc
                    ps, pb = acc()
                    for kc in range(8):
                        MM(ps[:, 0:T], slot[:, kc * 512 + fc * 128:kc * 512 + fc * 128 + 128], U[:, kc, 0:T], kc == 0, kc == 7,
                           [wb, b_U[kc]], [pb], inc=(kc == 7))
                    STT(H[:, oc, 0:T], ps[:, 0:T], vcol("o_b2%d" % i, oc), H[:, oc, 0:T], ALU.add, ALU.add, [pb, *CSA, b_H[oc]], [b_H[oc]])

        def even_mixer(T, i, ti):
            gb0, nb = tiles[ti]
            rmsnorm(T, "e_g%d" % i)
            sz_b = bigs(0, 4096)
            xs_b = bigs(4096, 8192)
            slot, wb = W.get()
            for fc in range(4):
                ps, pb = acc()
                for kc in range(8):
                    MM(ps[:, 0:T], slot[:, kc * 512 + fc * 128:kc * 512 + fc * 128 + 128], U[:, kc, 0:T], kc == 0, kc == 7, [wb, b_U[kc]], [pb], inc=(kc == 7))
                COPY("act", QK32[:, fc, 0:T], ps[:, 0:T], [pb], b_QK[2 * fc:2 * fc + 2])
            slot, wb = W.get()
            for kv in range(2):
                ps, pb = acc()
                for kc in range(8):
                    MM(ps[:, 0:T], slot[:, kc * 512 + kv * 128:kc * 512 + kv * 128 + 128], U[:, kc, 0:T], kc == 0, kc == 7, [wb, b_U[kc]], [pb], inc=(kc == 7))
                COPY("dve", QK32[:, 4 + kv, 0:T], ps[:, 0:T], [pb], b_QK[8 + 2 * kv:10 + 2 * kv])
            vd_ps, vd_pb = bank(3), b_ps[3]
            for b in range(nb):
                for kc in range(8):
                    MM(vd_ps[:, b * 128:b * 128 + 128 + 0], U[:, kc, b * 128:(b + 1) * 128], slot[:, kc * 512 + 256:kc * 512 + 384], kc == 0, kc == 7,
                       [wb, b_U[kc]], [vd_pb], inc=(kc == 7))
                COPY("act", VT[:, b, :], vd_ps[:, b * 128:b * 128 + 128], [vd_pb], [b_VT[b]])
            dt_ps, dt_pb = acc()
            for b in range(nb):
                for kc in range(8):
                    MM(dt_ps[:, b * 16:b * 16 + 16], U[:, kc, b * 128:(b + 1) * 128], slot[:, kc * 512 + 384:kc * 512 + 400], kc == 0, kc == 7,
                       [wb, b_U[kc]], [dt_pb], inc=(kc == 7))
            DTALL = RS[:, 1, 0:64].rearrange("p (b h) -> p b h", b=4)
            TT("dve", DTALL[:, 0:nb, :], dt_ps[:, 0:nb * 16].rearrange("p (b h) -> p b h", b=nb),
               ROWS[:, i * 40:i * 40 + 16].unsqueeze(1).to_broadcast([128, nb, 16]), ALU.add, [dt_pb, *CSA], [b_RS[1]])
            ACT(DTALL[:, 0:nb, :], DTALL[:, 0:nb, :], AF.Exp, [b_RS[1]], [b_RS[1]])
            ACT(DTALL[:, 0:nb, :], DTALL[:, 0:nb, :], AF.Ln, [b_RS[1]], [b_RS[1]], bias=1.0, scale=1.0)
            TT("dve", DTALL[:, 0:nb, :], DTALL[:, 0:nb, :], SDT[:, 0:nb].unsqueeze(2).to_broadcast([128, nb, 16]), ALU.mult, [b_RS[1], b_SDT], [b_RS[1]])
            for j in range(2):
                slot, wb = W.get()
                for fc in range(4):
                    c = j * 4 + fc
                    ps, pb = acc()
                    for kc in range(8):
                        MM(ps[:, 0:T], slot[:, kc * 512 + fc * 128:kc * 512 + fc * 128 + 128], U[:, kc, 0:T], kc == 0, kc == 7, [wb, b_U[kc]], [pb], inc=(kc == 7))
                    ACT(SZ[:, c, 0:T], ps[:, 0:T], AF.Silu, [pb], sz_b)
            COPY("pool", XP16[:, :, 0:3], XPC[:, i, :, :], [b_XPC[i]], b_XP)
            for j in range(3):
                slot, wb = W.get()
                for fc in range(4 if j < 2 else 2):
                    c = j * 4 + fc
                    ps, pb = acc()
                    for kc in range(8):
                        MM(ps[:, 0:T], slot[:, kc * 512 + fc * 128:kc * 512 + fc * 128 + 128], U[:, kc, 0:T], kc == 0, kc == 7, [wb, b_U[kc]], [pb], inc=(kc == 7))
                    TT("dve", XP16[:, c, 3:3 + T], ps[:, 0:T], SDT[:, 26:26 + T], ALU.mult, [pb, b_SDT], [b_XP[c]])
            cwb = VC["e_cw%d" % i]
            for c in range(10):
                ps, pb = acc()
                dg, dgb = dg_batch(VEC[:, cwb + c:cwb + c + 40:10], 4)
                for j in range(4):
                    MM(ps[:, 0:T], dg[:, j, :], XP16[:, c, j:j + T], j == 0, j == 3, [dgb, b_XP[c]], [pb], inc=(j == 3))
                if c < 8:
                    ACT(XS[:, c, 0:T], ps[:, 0:T], AF.Silu, [pb, *CSA], xs_b, bias=vcol("e_cb%d" % i, c), scale=1.0)
                else:
                    ACT(BC16[:, c - 8, 0:T], ps[:, 0:T], AF.Silu, [pb, *CSA], [b_BC[c - 8]], bias=vcol("e_cb%d" % i, c), scale=1.0)
            COPY("pool", XPC[:, i, :, :], XP16[:, :, T:T + 3], b_XP, [b_XPC[i]])
            for c in range(6):
                s = c % 4
                ACT(SQ[:, s, 0:T], QK32[:, c, 0:T], AF.Square, b_QK[2 * c:2 * c + 2], [b_SQ[s]])
                ps, pb = acc()
                MM(ps[:, 0:T], onesblk16, SQ[:, s, 0:T], True, True, [*CSA, b_SQ[s]], [pb], inc=True)
                rot = [(RS[:, 0, 0:T], b_RS[0]), (TMPF[:, 0, 0:T], b_TMPF[0]), (TMPF[:, 1, 0:T], b_TMPF[1])][c % 3]
                rap, rbf = rstd_from(ps, pb, T, 1.0 / 64, EPS, 0, dst=rot)
                if c < 4:
                    STT(Q16[:, c, 0:T], QK32[:, c, 0:T], vcol("e_qn%d" % i, c), rap, ALU.mult, ALU.mult, b_QK[2 * c:2 * c + 2] + [*CSA, rbf], [b_Q16[c]])
                else:
                    STT(K16[:, c - 4, 0:T], QK32[:, c, 0:T], vcol("e_kn%d" % i, 0), rap, ALU.mult, ALU.mult, b_QK[2 * c:2 * c + 2] + [*CSA, rbf], [b_K16[c - 4]])
            for c in range(8):
                ACT(G32[:, c, 0:T], XS[:, c, 0:T], AF.Identity, xs_b + CSA, [b_G[c], b_XT], scale=vcol("e_ds%d" % i, c))
            fmeta = SDT[:, 25:26]
            TT("dve", E16[:, 0, 0:256].rearrange("p (a k) -> p a k", a=2), K16[:, :, 384:512], KMETA[:, i, :, :], ALU.subtract, b_K16 + [b_KMETA[i]], [b_E16[0]])
            STT(KMETA[:, i, :, :], E16[:, 0, 0:256].rearrange("p (a k) -> p a k", a=2), fmeta, KMETA[:, i, :, :], ALU.mult, ALU.add, [b_E16[0], b_SDT, b_KMETA[i]], [b_KMETA[i]])
            TT("dve", E16[:, 1, 0:128], VT[:, 3, :], VMETA[:, i, :], ALU.subtract, [b_VT[3], b_VMETA[i]], [b_E16[1]])
            STT(VMETA[:, i, :], E16[:, 1, 0:128], fmeta, VMETA[:, i, :], ALU.mult, ALU.add, [b_E16[1], b_SDT, b_VMETA[i]], [b_VMETA[i]])
            w0 = PSD[2]
            w1 = PSD[3]
            w0b = b_ps[4:6]
            w1b = b_ps[6:8]
            w0v = w0[:, :].rearrange("p (h q) -> p h q", h=8)
            w1v = w1[:, :].rearrange("p (h q) -> p h q", h=8)
            sched = [('A', 0), ('B', 0)]
            for b_ in range(nb):
                if b_ + 1 < nb:
                    sched += [('A', b_ + 1), ('C', b_), ('B', b_ + 1), ('D', b_)]
                else:
                    sched += [('C', b_), ('D', b_)]
            for sec, b in sched:
                gb = gb0 + b
                cols = slice(b * 128, (b + 1) * 128)
                kbs = [(lambda kv: KMETA[:, i, kv, :], [b_KMETA[i]], VMETA[:, i, :], [b_VMETA[i]], -1)]
                if b == 0:
                    kbs.append((lambda kv: KPREV[:, i, kv, :], [b_KPREV[i]], VPREV[:, i, :], [b_VPREV[i]], 3))
                else:
                    kbs.append((lambda kv, b=b: K16[:, kv, (b - 1) * 128:b * 128], b_K16, VT[:, b - 1, :], [b_VT[b - 1]], 3))
                kbs.append((lambda kv, b=b: K16[:, kv, b * 128:(b + 1) * 128], b_K16, VT[:, b, :], [b_VT[b]], 4))

                def coef(t, b=b):
                    return SDT[:, 4 + b * 5 + t:4 + b * 5 + t + 1]
                nk = len(kbs)
                DT = DTALL[:, b, :]
                ADT = SM[:, 0, :]
                NCS = SM[:, 1, :]
                EL = SM[:, 2, :]
                if sec == 'A':
                    TS("dve", TME[:, :], TABv[:, 0, :, :].rearrange("p h q -> p (h q)"), coef(0), None, ALU.mult, None, [b_SDT, *CSA], [b_TME])
                    for t in (1, 2):
                        STT(TME[:, :], TABv[:, t, :, :].rearrange("p h q -> p (h q)"), coef(t), TME[:, :], ALU.mult, ALU.add, [b_SDT, b_TME, *CSA], [b_TME])
                    for ki, (kf, kbuf, vap, vbuf, tabi) in enumerate(kbs):
                        s = ki % 2
                        ws, wsb = (w1, w1b) if ki == 1 else (w0, w0b)
                        for h in range(8):
                            kv, c, half = h // 4, h // 2, h % 2
                            hp = half * 4 + c
                            MM(ws[:, hp * 128:(hp + 1) * 128], kf(kv)[half * 64:half * 64 + 64, :], Q16[half * 64:half * 64 + 64, c, cols], True, True,
                               kbuf + [b_Q16[c]], wsb, inc=(h == 7))
                        for hb in range(2):
                            ACT(E16[:, s, hb * 512:(hb + 1) * 512], ws[:, hb * 512:(hb + 1) * 512], AF.Exp, wsb, [b_E16[s]], scale=0.125)
                        if tabi < 0:
                            TT("dve", PT[:, ki, :], E16[:, s, :], TME[:, :], ALU.mult, [b_E16[s], b_TME], [b_PT[ki]])
                        else:
                            STT(PT[:, ki, :], E16[:, s, :], coef(tabi), TABv[:, tabi, :, :].rearrange("p h q -> p (h q)"), ALU.mult, ALU.mult, [b_E16[s], b_SDT, *CSA], [b_PT[ki]])
                if sec == 'B':
                    ot_ps, ot_pb = acc()
                    for hh in range(2):
                        for ki in range(nk):
                            MM(w1[:, hh * 512:(hh + 1) * 512], ones16, PT[:, ki, hh * 512:(hh + 1) * 512], ki == 0, ki == nk - 1, [*CSA, b_PT[ki]], w1b,
                               inc=(hh == 1 and ki == nk - 1))
                    for h in range(8):
                        kv, c, half = h // 4, h // 2, h % 2
                        for ki, (kf, kbuf, vap, vbuf, tabi) in enumerate(kbs):
                            hp = half * 4 + c
                            MM(ot_ps[half * 64:half * 64 + 64, c * 128:(c + 1) * 128], vap[:, kv * 64:kv * 64 + 64], PT[:, ki, hp * 128:(hp + 1) * 128],
                               ki == 0, ki == nk - 1, vbuf + [b_PT[ki]], [ot_pb], inc=(h == 7 and ki == nk - 1), tp=(0, half * 64))
                    RDv = RD[:, :].rearrange("p (c q) -> p c q", c=4)
                    for half in range(2):
                        pr = slice(half * 64, half * 64 + 64)
                        TT("dve", RDv[pr, :, :], w1v[pr, half * 4:half * 4 + 4, :],
                           ESK[pr, i, half::2].unsqueeze(2).to_broadcast([64, 4, 128]), ALU.add, w1b + CSA, [b_RD])
                    ACT(RD[:, :], RD[:, :], AF.Ln, [b_RD], [b_RD])
                    ACT(RD[:, :], RD[:, :], AF.Exp, [b_RD], [b_RD], scale=-1.0)
                    TT("dve", MIX[:, 0:4, cols], ot_ps[:, 0:512].rearrange("p (c q) -> p c q", c=4), RDv[:, :, :], ALU.mult, [ot_pb, b_RD], b_QK[0:4])
                if sec == 'C':
                    TT("dve", ADT, DT, AROW[:, i, :], ALU.mult, [b_RS[1], *CSA], [b_SM[0]])
                    for half in range(2):
                        ps, pb = acc()
                        for j in range(4):
                            c = half * 4 + j
                            TR(ps[:, j * 128:(j + 1) * 128], XS[:, c, cols], ident, CSA + xs_b, [pb], inc=(j == 3))
                        COPY("act", XTM[:, half * 512:(half + 1) * 512], ps[:, 0:512], [pb], [b_XTM])
                    st_ps, st_pb = bank(3), b_ps[3]
                    P.op("pe", lambda e, cols=cols: e.transpose(bank(3)[:, 256:320].bitcast(BF16), BC16[:, 0, cols], ident16), [*CSA, b_BC[0]], [st_pb])
                    COPY("act", BTM[:, :], st_ps[:, 256:320].bitcast(BF16), [st_pb], [b_BTM])
                    MM(st_ps[:, 0:16], tri, ADT, True, True, [*CSA, b_SM[0]], [st_pb], inc=False)
                    MM(st_ps[:, 16:32], ones32, ADT, True, True, [*CSA, b_SM[0]], [st_pb], inc=True)
                    TS("dve", NCS, st_ps[:, 0:16], -1.0, None, ALU.mult, None, [st_pb], [b_SM[1]])
                    TT("dve", EL, NCS, st_ps[:, 16:32], ALU.add, [st_pb, b_SM[1]], [b_SM[2]])
                    ACT(EL, EL, AF.Exp, [b_SM[2]], [b_SM[2]])
                    ACT(DEC2[0:64, :], st_ps[0:64, 16:24], AF.Exp, [st_pb], [b_DEC2])
                    ACT(DEC2[64:128, :], st_ps[64:128, 24:32], AF.Exp, [st_pb], [b_DEC2])
                    XTMv = XTM[:, :].rearrange("p (h d) -> p h d", h=16)
                    TT("dve", XDT[:, :].rearrange("p (h d) -> p h d", h=16), XTMv, DT.unsqueeze(2).to_broadcast([128, 16, 64]), ALU.mult, [b_XTM, b_RS[1]], [b_XDT])
                    TT("dve", XW[:, :].rearrange("p (h d) -> p h d", h=16), XDT[:, :].rearrange("p (h d) -> p h d", h=16),
                       EL.unsqueeze(2).to_broadcast([128, 16, 64]), ALU.mult, [b_XDT, b_SM[2]], [b_XW])
                if sec == 'D':
                    cbs = [acc(), acc()]
                    for g in range(2):
                        pr = slice(g * 64, g * 64 + 64)
                        MM(cbs[g][0][:, 0:128], BC16[pr, 0, cols], BC16[pr, 1, cols], True, True, b_BC, [cbs[g][1]], inc=True)
                    for j in range(8):
                        MM(w1[0:64, j * 128:(j + 1) * 128], ADT[:, j:j + 1].to_broadcast([128, 64]), tri, True, True, [b_SM[0], *CSA], w1b, inc=False)
                        MM(w1[64:128, j * 128:(j + 1) * 128], ADT[:, 8 + j:9 + j].to_broadcast([128, 64]), tri, True, True, [b_SM[0], *CSA], w1b, inc=(j == 7), tp=(0, 64))
                    for hb in range(2):
                        ACT(E16[:, 0, hb * 512:(hb + 1) * 512], w1[:, hb * 512:(hb + 1) * 512], AF.Exp, w1b, [b_E16[0]])
                    TT("dve", C2[:, :].rearrange("p (j l) -> p j l", j=8), E16[:, 0, :].rearrange("p (j l) -> p j l", j=8),
                       BC16[:, 1, cols].unsqueeze(1).to_broadcast([128, 8, 128]), ALU.mult, [b_E16[0], b_BC[1]], [b_C2])
                    wg = [(w0, w0b), (w1, w1b)]
                    for g in range(2):
                        for j in range(8):
                            h = g * 8 + j
                            MM(wg[g][0][:, j * 128:(j + 1) * 128], ADT[:, h:h + 1].to_broadcast([128, 128]), tri, True, False, [b_SM[0], *CSA], wg[g][1], inc=False)
                            MM(wg[g][0][:, j * 128:(j + 1) * 128], ident16, neg16, False, True, CSA, wg[g][1], inc=(j == 7))
                    for g in range(2):
                        eb = 1 - g
                        for j in range(8):
                            h = g * 8 + j
                            ACT(E16[:, eb, j * 128:(j + 1) * 128], wg[g][0][:, j * 128:(j + 1) * 128], AF.Exp, wg[g][1] + [b_SM[1]], [b_E16[eb]], bias=NCS[:, h:h + 1], scale=1.0)
                        TT("dve", WT[:, g, :].rearrange("p (j l) -> p j l", j=8), E16[:, eb, :].rearrange("p (j l) -> p j l", j=8),
                           cbs[g][0][:, 0:128].unsqueeze(1).to_broadcast([128, 8, 128]), ALU.mult, [b_E16[eb], cbs[g][1]], [b_WT[g]])
                    for c in range(8):
                        g = c // 4
                        for half in range(2):
                            h = 2 * c + half
                            j = h % 8
                            out = w0[half * 64:half * 64 + 64, c * 128:(c + 1) * 128]
                            MM(out, XDT[:, h * 64:(h + 1) * 64], WT[:, g, j * 128:(j + 1) * 128], True, False, [b_XDT, b_WT[g]], w0b, inc=False, tp=(0, half * 64))
                            MM(out, ST16[g * 64:g * 64 + 64, i, j * 64:(j + 1) * 64], C2[g * 64:g * 64 + 64, j * 128:(j + 1) * 128], False, True,
                               [b_ST16[i], b_C2], w0b, inc=(c == 7 and half == 1), tp=(g * 64, half * 64))
                    for hb in range(2):
                        TT("dve", G32[:, 4 * hb:4 * hb + 4, cols], w0v[:, 4 * hb:4 * hb + 4, :], G32[:, 4 * hb:4 * hb + 4, cols], ALU.add, w0b + b_G, b_G)
                    TT("pool", G32[:, :, cols], G32[:, :, cols], SZ[:, :, cols], ALU.mult, b_G + sz_b, b_G)
                    sn_ps, sn_pb = acc()
                    for g in range(2):
                        MM(sn_ps[g * 64:g * 64 + 64, 0:512], BTM[:, g * 64:g * 64 + 64], XW[:, g * 512:(g + 1) * 512], True, True, [b_BTM, b_XW], [sn_pb], inc=(g == 1), tp=(0, g * 64))
                    STv = ST32[:, i, :].rearrange("p (j d) -> p j d", j=8)
                    TT("dve", STv, STv, DEC2[:, :].unsqueeze(2).to_broadcast([128, 8, 64]), ALU.mult, [b_ST32[i], b_DEC2], [b_ST32[i]])
                    TT("dve", ST32[:, i, :], ST32[:, i, :], sn_ps[:, 0:512], ALU.add, [b_ST32[i], sn_pb], [b_ST32[i]])
                    COPY("act", ST16[:, i, :], ST32[:, i, :], [b_ST32[i]], [b_ST16[i]])

            COPY("pool", KPREV[:, i, :, :], K16[:, :, (nb - 1) * 128:nb * 128], b_K16, [b_KPREV[i]])
            COPY("pool", VPREV[:, i, :], VT[:, nb - 1, :], [b_VT[nb - 1]], [b_VPREV[i]])
            for g in range(2):
                ps, pb = acc()
                for k in range(4):
                    c = g * 4 + k
                    s = c % 4
                    ACT(SQ[:, s, 0:T], G32[:, c, 0:T], AF.Square, [b_G[c]], [b_SQ[s]])
                    MM(ps[:, 0:T], ones16, SQ[:, s, 0:T], k == 0, k == 3, [*CSA, b_SQ[s]], [pb], inc=True)
                rstd_from(ps, pb, T, 1.0 / 512, EPS, 0)
                for k in range(4):
                    c = g * 4 + k
                    STT(MIX[:, 4 + c, 0:T], G32[:, c, 0:T], vcol("e_nw%d" % i, c), RS[:, 0, 0:T], ALU.mult, ALU.mult, [b_G[c], *CSA, b_RS[0]], [b_QK[4 + c]])
            for j in range(4):
                slot, wb = W.get()
                for fc in range(2):
                    oc = j * 2 + fc
                    ps, pb = acc()
                    for kc in range(12):
                        MM(ps[:, 0:T], slot[:, kc * 256 + fc * 128:kc * 256 + fc * 128 + 128], MIX[:, kc, 0:T], kc == 0, kc == 11, [wb, b_QK[kc]], [pb], inc=(kc == 11))
                    TT("dve", H[:, oc, 0:T], H[:, oc, 0:T], ps[:, 0:T], ALU.add, [pb, b_H[oc]], [b_H[oc]])

        MEMSET("pool", XPG[:, :], 0.0, ALLX)
        P.dma("sp", "xdz", lambda e: e.dma_start(out=xdst[0:512, :].rearrange("(b p) f -> p b f", p=128), in_=XT[:, :, :]), [b_XT], [b_DST])
        load_x(0)
        for ti, (b0, nb) in enumerate(tiles):
            T = nb * 128
            load_sdat(ti)
            load_exchange(ti)
            transpose_in(ti)
            even_mixer(T, 0, ti)
            mlp(T, 0)
            if ti + 1 < len(tiles):
                load_x(ti + 1)
            conformer(T, 0, ti)
            mlp(T, 1)
            store_out(ti)
        P.wait_all("sp", [b_yout])
        P.emit()
    return nc


def _piece_k1024(w, cols):
    sel = np.zeros((1024, 512), np.float32)
    sel[:, :len(cols)] = w[:, cols]
    return np.ascontiguousarray(sel.reshape(8, 128, 512).transpose(1, 0, 2).reshape(128, 4096))


def _pack_weights(stage, w_in, w_out, pw1_w, pw2_w, w_up, w_down):
    pieces = []
    r = np.arange

    def mlp_pieces(l):
        for j in range(8):
            pieces.append(_piece_k1024(w_up[l], r(j * 512, (j + 1) * 512)))
        for oc in range(8):
            blk = w_down[l][:, oc * 128:(oc + 1) * 128]
            pieces.append(np.ascontiguousarray(blk.reshape(32, 128, 128).transpose(1, 0, 2).reshape(128, 4096)))

    for l in (2 * stage, 2 * stage + 1):
        i = l // 2
        if l % 2 == 0:
            w = w_in[i]
            pieces.append(_piece_k1024(w, r(0, 512)))
            a1 = np.concatenate([r(512, 576), r(512, 576), r(576, 640), r(576, 640), r(640, 768), r(3072, 3088)])
            pieces.append(_piece_k1024(w, a1))
            pieces.append(_piece_k1024(w, r(768, 1280)))
            pieces.append(_piece_k1024(w, r(1280, 1792)))
            pieces.append(_piece_k1024(w, r(1792, 2304)))
            pieces.append(_piece_k1024(w, r(2304, 2816)))
            pieces.append(_piece_k1024(w, r(2816, 3072)))
            wo = w_out[i]
            for j in range(4):
                blk = wo[:, j * 256:(j + 1) * 256].reshape(12, 128, 256).transpose(1, 0, 2).reshape(128, 3072)
                p = np.zeros((128, 4096), np.float32)
                p[:, :3072] = blk
                pieces.append(p)
        else:
            for j in range(4):
                pieces.append(_piece_k1024(pw1_w[i], r(j * 512, (j + 1) * 512)))
            for j in range(2):
                pieces.append(_piece_k1024(pw2_w[i], r(j * 512, (j + 1) * 512)))
        mlp_pieces(l)
    out = np.stack(pieces, 0)
    assert out.shape[0] == PIECES_PER_TILE, out.shape
    return out


def _fm(v):
    v = np.asarray(v, np.float32)
    return v.reshape(-1, 128).T


def _pack_vecs(inp0, stage):
    inp = {}
    for k, v in inp0.items():
        v = np.asarray(v)
        if k in ("x", "meta_tokens"):
            continue
        if k in ("mlp_norm", "w_up", "w_down"):
            inp[k] = np.concatenate([v[2 * stage:2 * stage + 2], v[2 * stage:2 * stage + 2]], 0) if k == "mlp_norm" else None
        elif k in ("w_in", "w_out", "pw1_w", "pw2_w"):
            inp[k] = None
        else:
            inp[k] = np.stack([v[stage], v[stage]], 0)
    V = np.zeros((128, NV), np.float32)

    def put(name, arr):
        arr = np.asarray(arr, np.float32)
        V[:, VC[name]:VC[name] + arr.shape[1]] = arr

    for i in range(2):
        put("e_g%d" % i, _fm(inp["mix_norm_even"][i]))
        cw = inp["ssm_conv_w"][i]
        put("e_cw%d" % i, np.concatenate([_fm(cw[j]) for j in range(4)], 1))
        put("e_cb%d" % i, _fm(inp["ssm_conv_b"][i]))
        put("e_ds%d" % i, _fm(np.repeat(inp["d_skip"][i], 64)))
        put("e_nw%d" % i, _fm(inp["ssm_norm_w"][i]))
        put("e_qn%d" % i, _fm(np.tile(inp["q_norm"][i], 8)))
        put("e_kn%d" % i, _fm(np.tile(inp["k_norm"][i], 2)))
        put("o_g%d" % i, _fm(inp["mix_norm_odd"][i]))
        put("o_b1%d" % i, _fm(inp["pw1_b"][i]))
        dw = inp["dw_w"][i]
        put("o_dw%d" % i, np.concatenate([_fm(dw[j]) for j in range(31)], 1))
        put("o_db%d" % i, _fm(inp["dw_b"][i]))
        put("o_lg%d" % i, _fm(inp["ln_g"][i]))
        put("o_lb%d" % i, _fm(inp["ln_b"][i]))
        put("o_b2%d" % i, _fm(inp["pw2_b"][i]))
    for l in range(4):
        put("m_g%d" % l, _fm(inp["mlp_norm"][l]))
    R = np.zeros((128, NR), np.float32)
    for i in range(2):
        R[:, i * 40:i * 40 + 16] = np.asarray(inp["dt_bias"][i], np.float32)[None, :]
        R[:, i * 40 + 16:i * 40 + 32] = np.asarray(inp["a_log"][i], np.float32)[None, :]
        R[:, i * 40 + 32:i * 40 + 40] = np.asarray(inp["sinks"][i], np.float32)[None, :]
    return V, R


def _consts():
    C = np.zeros((128, NCST), np.float32)
    C[:, 0:128] = np.eye(128, dtype=np.float32)
    s = np.arange(128)[:, None]
    l = np.arange(128)[None, :]
    C[:, 128:256] = (s <= l)
    C[:, 256:384] = 1.0
    C[:, 384:512] = ((s // 64) == (l // 64))
    C[:, 512:640] = np.where(l < s, -30000.0, 0.0)
    slopes = np.exp2(-8.0 * np.arange(1, 9, dtype=np.float64) / 8)
    k = np.arange(128)[:, None].astype(np.float64)
    q = np.arange(128)[None, :].astype(np.float64)
    T = np.zeros((128, 5, 8, 128), np.float64)
    for hp in range(8):
        h = hp
        sl = slopes[2 * (hp % 4) + hp // 4]
        valid = (k >= 112) & (q >= 112) & (k <= q)
        T[:, 0, h, :] = np.where(valid, np.exp(-sl * (q - k)), 0.0)
        valid = (k >= 112) & (q >= 0)
        T[:, 1, h, :] = np.where(valid, np.exp(-sl * np.minimum(q + 16 - (k - 112), 128)), 0.0)
        T[:, 2, h, :] = np.where(k >= 112, np.exp(-sl * 128.0), 0.0)
        T[:, 3, h, :] = np.where(k > q, np.exp(-sl * (q + 128 - k)), 0.0)
        T[:, 4, h, :] = np.where(k <= q, np.exp(-sl * (q - k)), 0.0)
    return C, T.reshape(128, NTAB).astype(np.float32)


_NC_CACHE = {}


def _step_data(nsteps, lag=0):
    S = np.zeros((nsteps, 128, SD), np.float32)
    p = np.arange(128)
    for s_ in range(nsteps):
        t = s_ - lag
        for b in range(4):
            gb = -3 + 4 * t + b
            if t < 0 or gb < 0:
                valid = np.zeros(128, np.float32)
                co = [0, 0, 0, 0, 0]
            elif gb == 0:
                valid = (p >= 112).astype(np.float32)
                co = [1, 0, 0, 0, 0]
            elif gb == 1:
                valid = np.ones(128, np.float32)
                co = [0, 1, 0, 0, 1]
            else:
                valid = np.ones(128, np.float32)
                co = [0, 0, 1, 1, 1]
            S[s_, :, b] = valid
            S[s_, :, 4 + b * 5:4 + b * 5 + 5] = np.asarray(co, np.float32)[None, :]
            S[s_, :, 26 + b * 128:26 + (b + 1) * 128] = valid[None, :]
        S[s_, :, 24] = 1.0 if lag > 0 else 0.0
        S[s_, :, 25] = 1.0 if t == 0 else 0.0
    return S


def _run(inputs, nsteps, nbatch, nlayers=4):
    key = nsteps
    if key not in _NC_CACHE:
        _NC_CACHE[key] = build(nsteps)
    nc = _NC_CACHE[key]
    x = np.asarray(inputs["x"], np.float32)
    meta = np.asarray(inputs["meta_tokens"], np.float32)
    C, TB = _consts()
    in_maps = []
    per_stage = []
    for stage in range(2):
        wall = _pack_weights(stage, *[np.asarray(inputs[k], np.float32) for k in ("w_in", "w_out", "pw1_w", "pw2_w", "w_up", "w_down")])
        V, R = _pack_vecs(inputs, stage)
        per_stage.append((wall, V, R, _step_data(nsteps, stage)))
    zeros = np.zeros((nsteps * 512, D), np.float32)
    for b in range(nbatch):
        xin = np.zeros((nsteps * 512, D), np.float32)
        xin[3 * 128 + 112:4 * 128] = meta
        xin[4 * 128:4 * 128 + x.shape[1]] = x[b]
        for stage in range(2):
            wall, V, R, SDA = per_stage[stage]
            in_maps.append({"xin": xin if stage == 0 else zeros, "wall": wall, "vecs": V, "rows": R, "cst": C, "tab": TB, "sdat": SDA})
    res = run_bass_kernel_spmd(nc, in_maps, core_ids=list(range(2 * nbatch)))
    return np.stack([res.results[2 * b + 1]["y"][8 * 128:8 * 128 + x.shape[1]] for b in range(nbatch)], 0)


def kernel(**inputs):
    x = inputs["x"]
    bsz, seq, _ = x.shape
    nsteps = (seq // 128 + 4) // 4 + 1
    return _run(inputs, nsteps, bsz).astype(np.float32)
```
